# Optimizing a Trainium2 kernel written in Bass

```python
import math
import jax, jax.numpy as jnp
from jax import lax
import numpy as np

D_MODEL = 4096
BATCH = 4
SEQ = 4096
DEPTH = 1

HEAD_DIM = 128
ATTN_GROUPS = ((128, 1), (512, 4), (2048, 16))
N_ATTN_GROUPS = len(ATTN_GROUPS)
HEADS_PER_GROUP = D_MODEL // 512
ATTN_WIDTH = HEADS_PER_GROUP * HEAD_DIM
ATTN_QKV_WIDTH = N_ATTN_GROUPS * ATTN_WIDTH
BAND_BLOCK = 128
SGU_WIDTH = D_MODEL // 2
SGU_GROUP_CH = 128
SGU_GROUPS = SGU_WIDTH // SGU_GROUP_CH
CHUNK = 128
IN_WIDTH = 3 * ATTN_QKV_WIDTH + 2 * SGU_WIDTH
ROPE_THETA = 500000.0
ROT_DIM = HEAD_DIM // 4
XA_HEADS = 4
XA_WIDTH = XA_HEADS * HEAD_DIM
N_MEM = 256
D_FF = 4 * D_MODEL
EPS = 1e-6
NEG = -1e30

kernel_name = "hybrid_dilated_attn_gmlp_gated_block"


def rms_norm(x, g):
    x32 = x.astype(jnp.float32)
    y = x32 * lax.rsqrt(jnp.mean(x32 * x32, axis=-1, keepdims=True) + EPS)
    return (y * g.astype(jnp.float32)).astype(x.dtype)


def layer_norm(x, g, b):
    x32 = x.astype(jnp.float32)
    mu = jnp.mean(x32, axis=-1, keepdims=True)
    var = jnp.mean(jnp.square(x32 - mu), axis=-1, keepdims=True)
    y = (x32 - mu) * lax.rsqrt(var + EPS)
    return (y * g.astype(jnp.float32) + b.astype(jnp.float32)).astype(x.dtype)


def rope_tables(positions):
    inv = ROPE_THETA ** (-jnp.arange(0, ROT_DIM, 2, dtype=jnp.float32) / ROT_DIM)
    ang = positions.astype(jnp.float32)[..., None] * inv
    return jnp.cos(ang), jnp.sin(ang)


def apply_partial_rope(t, cos, sin):
    c = cos[:, :, None, None, :].astype(t.dtype)
    s = sin[:, :, None, None, :].astype(t.dtype)
    half = ROT_DIM // 2
    t1, t2, rest = t[..., :half], t[..., half:ROT_DIM], t[..., ROT_DIM:]
    return jnp.concatenate([t1 * c - t2 * s, t2 * c + t1 * s, rest], axis=-1)


def banded_causal_attention(q, k, v, n_back):
    N, L, H, hd = q.shape
    nb = -(-L // BAND_BLOCK)
    Lp = nb * BAND_BLOCK
    pad = Lp - L
    qb = jnp.pad(q, ((0, 0), (0, pad), (0, 0), (0, 0))).reshape(N, nb, BAND_BLOCK, H, hd)
    kp = jnp.pad(k, ((0, 0), (BAND_BLOCK, pad), (0, 0), (0, 0))).reshape(N, nb + 1, BAND_BLOCK, H, hd)
    vp = jnp.pad(v, ((0, 0), (BAND_BLOCK, pad), (0, 0), (0, 0))).reshape(N, nb + 1, BAND_BLOCK, H, hd)
    kw = jnp.concatenate([kp[:, :-1], kp[:, 1:]], axis=2)
    vw = jnp.concatenate([vp[:, :-1], vp[:, 1:]], axis=2)
    qi = jnp.arange(BAND_BLOCK)[:, None]
    kj = jnp.arange(2 * BAND_BLOCK)[None, :]
    dist = qi + BAND_BLOCK - kj
    keypos = jnp.arange(nb)[:, None, None] * BAND_BLOCK - BAND_BLOCK + kj[None]
    mask = (dist >= 0)[None] & (dist <= n_back)[None] & (keypos >= 0)
    s = jnp.einsum('nbqhd,nbkhd->nbhqk', qb, kw).astype(jnp.float32)
    s = jnp.where(mask[None, :, None], s, NEG)
    lse = jax.nn.logsumexp(s, axis=-1)
    p = jnp.exp(s - lse[..., None])
    o = jnp.einsum('nbhqk,nbkhd->nbqhd', p.astype(v.dtype), vw)
    o = o.reshape(N, Lp, H, hd)[:, :L]
    lse = lse.transpose(0, 1, 3, 2).reshape(N, Lp, H)[:, :L]
    return o, lse


def dilated_causal_attention(q, k, v, window, dilation):
    B, S, H, hd = q.shape
    L = S // dilation

    def to_res(t):
        return t.reshape(B, L, dilation, H, hd).transpose(0, 2, 1, 3, 4).reshape(B * dilation, L, H, hd)

    o, lse = banded_causal_attention(to_res(q), to_res(k), to_res(v), window // dilation)
    o = o.reshape(B, dilation, L, H, hd).transpose(0, 2, 1, 3, 4).reshape(B, S, H, hd)
    lse = lse.reshape(B, dilation, L, H).transpose(0, 2, 1, 3).reshape(B, S, H)
    return o, lse


def setup_inputs(seed: int = 0) -> dict:
    key = jax.random.key(seed)
    ks = jax.random.split(key, 32)
    f32 = jnp.float32

    def nrm(k, shape, fan_in):
        return jax.random.normal(k, shape, f32) * (fan_in ** -0.5)

    def gain(k, shape):
        return 1.0 + 0.05 * jax.random.normal(k, shape, f32)

    x = jax.random.normal(ks[0], (BATCH, SEQ, D_MODEL), f32)
    mem = jax.random.normal(ks[1], (BATCH, N_MEM, D_MODEL), f32)
    offset = jax.random.randint(ks[2], (BATCH, 1), 0, 1024, dtype=jnp.int32)
    positions = offset + jnp.arange(SEQ, dtype=jnp.int32)[None, :]
    return {
        "x": x,
        "mem": mem,
        "positions": positions,
        "mix_pre_g": gain(ks[3], (DEPTH, D_MODEL)),
        "w_in": nrm(ks[4], (DEPTH, D_MODEL, IN_WIDTH), D_MODEL),
        "sgu_ln_g": gain(ks[5], (DEPTH, SGU_WIDTH)),
        "sgu_ln_b": 0.01 * jax.random.normal(ks[6], (DEPTH, SGU_WIDTH), f32),
        "w_spatial": nrm(ks[7], (DEPTH, SGU_GROUPS, CHUNK, CHUNK), CHUNK),
        "b_spatial": 1.0 + 0.01 * jax.random.normal(ks[8], (DEPTH, SGU_GROUPS, CHUNK), f32),
        "w_branch_a": nrm(ks[9], (DEPTH, ATTN_WIDTH, D_MODEL), ATTN_WIDTH),
        "w_branch_b": nrm(ks[10], (DEPTH, SGU_WIDTH, D_MODEL), SGU_WIDTH),
        "w_gate": nrm(ks[11], (DEPTH, D_MODEL, 2 * D_MODEL), D_MODEL),
        "b_gate": 0.01 * jax.random.normal(ks[12], (DEPTH, 2 * D_MODEL), f32),
        "w_out": nrm(ks[13], (DEPTH, D_MODEL, D_MODEL), D_MODEL),
        "mix_post_g": gain(ks[14], (DEPTH, D_MODEL)),
        "xa_pre_g": gain(ks[15], (DEPTH, D_MODEL)),
        "mem_norm_g": gain(ks[16], (DEPTH, D_MODEL)),
        "w_xq": nrm(ks[17], (DEPTH, D_MODEL, XA_WIDTH), D_MODEL),
        "w_xk": nrm(ks[18], (DEPTH, D_MODEL, XA_WIDTH), D_MODEL),
        "w_xv": nrm(ks[19], (DEPTH, D_MODEL, XA_WIDTH), D_MODEL),
        "w_xo": nrm(ks[20], (DEPTH, XA_WIDTH, D_MODEL), XA_WIDTH),
        "xa_post_g": gain(ks[21], (DEPTH, D_MODEL)),
        "mlp_pre_g": gain(ks[22], (DEPTH, D_MODEL)),
        "w_up": nrm(ks[23], (DEPTH, D_MODEL, D_FF), D_MODEL),
        "w_down": nrm(ks[24], (DEPTH, D_FF, D_MODEL), D_FF),
        "mlp_post_g": gain(ks[25], (DEPTH, D_MODEL)),
    }


def reference(x, mem, positions, mix_pre_g, w_in, sgu_ln_g, sgu_ln_b, w_spatial, b_spatial,
              w_branch_a, w_branch_b, w_gate, b_gate, w_out, mix_post_g, xa_pre_g, mem_norm_g,
              w_xq, w_xk, w_xv, w_xo, xa_post_g, mlp_pre_g, w_up, w_down, mlp_post_g):
    B, S, _ = x.shape
    M = mem.shape[1]
    dt = x.dtype
    scale = HEAD_DIM ** -0.5
    cos, sin = rope_tables(positions)
    causal_tri = jnp.tril(jnp.ones((CHUNK, CHUNK), dtype=dt))

    for l in range(DEPTH):
        h = rms_norm(x, mix_pre_g[l])
        proj = h @ w_in[l]
        q_all, k_all, v_all, u_b, v_b = jnp.split(
            proj, [ATTN_QKV_WIDTH, 2 * ATTN_QKV_WIDTH, 3 * ATTN_QKV_WIDTH,
                   3 * ATTN_QKV_WIDTH + SGU_WIDTH], axis=-1)
        gshape = (B, S, N_ATTN_GROUPS, HEADS_PER_GROUP, HEAD_DIM)
        q_all = apply_partial_rope(q_all.reshape(gshape), cos, sin) * jnp.asarray(scale, dt)
        k_all = apply_partial_rope(k_all.reshape(gshape), cos, sin)
        v_all = v_all.reshape(gshape)

        outs, lses = [], []
        for g, (window, dilation) in enumerate(ATTN_GROUPS):
            o, lse = dilated_causal_attention(q_all[:, :, g], k_all[:, :, g], v_all[:, :, g],
                                              window, dilation)
            outs.append(o)
            lses.append(lse)
        alpha = jax.nn.softmax(jnp.stack(lses, axis=0), axis=0)
        y_a = jnp.sum(alpha[..., None] * jnp.stack(outs, axis=0).astype(jnp.float32), axis=0)
        y_a = y_a.astype(dt).reshape(B, S, ATTN_WIDTH)

        u_b = jax.nn.gelu(u_b)
        v_b = layer_norm(jax.nn.gelu(v_b), sgu_ln_g[l], sgu_ln_b[l])
        vc = v_b.reshape(B, S // CHUNK, CHUNK, SGU_GROUPS, SGU_GROUP_CH)
        ws = w_spatial[l] * causal_tri
        mixed = jnp.einsum('gij,bnjgc->bnigc', ws, vc) + b_spatial[l].T[None, None, :, :, None]
        y_b = u_b * mixed.reshape(B, S, SGU_WIDTH)

        gates = jax.nn.sigmoid(h @ w_gate[l] + b_gate[l])
        g_a, g_b = jnp.split(gates, 2, axis=-1)
        merged = g_a * (y_a @ w_branch_a[l]) + g_b * (y_b @ w_branch_b[l])
        x = x + rms_norm(merged @ w_out[l], mix_post_g[l])

        h = rms_norm(x, xa_pre_g[l])
        m = rms_norm(mem, mem_norm_g[l])
        q = (h @ w_xq[l]).reshape(B, S, XA_HEADS, HEAD_DIM) * jnp.asarray(scale, dt)
        k = (m @ w_xk[l]).reshape(B, M, XA_HEADS, HEAD_DIM)
        v = (m @ w_xv[l]).reshape(B, M, XA_HEADS, HEAD_DIM)
        p = jax.nn.softmax(jnp.einsum('bshd,bmhd->bhsm', q, k).astype(jnp.float32), axis=-1)
        o = jnp.einsum('bhsm,bmhd->bshd', p.astype(dt), v).reshape(B, S, XA_WIDTH)
        x = x + rms_norm(o @ w_xo[l], xa_post_g[l])

        h = rms_norm(x, mlp_pre_g[l])
        a = jnp.square(jax.nn.relu(h @ w_up[l]))
        x = x + rms_norm(a @ w_down[l], mlp_post_g[l])
    return x
```

```python
import numpy as np
import ml_dtypes
from contextlib import ExitStack
import concourse.bass as bass
import concourse.mybir as mybir
from concourse.bass_utils import run_bass_kernel_spmd

F32 = mybir.dt.float32
BF16 = mybir.dt.bfloat16
I32 = mybir.dt.int32
AF = mybir.ActivationFunctionType
ALU = mybir.AluOpType

D = 4096
TT = 512
NTB = TT // 128
OWN = 2048
HALO = 2048
NT = OWN // TT
NHT = HALO // TT
GROUPS = ((128, 1), (512, 4), (2048, 16))
INW = 13312
DFF = 16384
SEC = 1024
EPS = 1e-6
NEGM = -30000.0
SCALE = 128 ** -0.5
RING = 4
POOL_ENG = "pool"
PI = float(np.pi)
TWO_PI = float(2 * np.pi)


class Eng:
    def __init__(self, name, sem, selfsync):
        self.name, self.sem, self.cnt, self.prog, self.seen, self.selfsync = name, sem, 0, [], {}, selfsync


class DSem:
    def __init__(self, sem):
        self.sem, self.count = sem, 0


class V:
    def __init__(self, ap, space, lo, hi):
        self.ap, self.space, self.lo, self.hi = ap, space, lo, hi


class Tracker:
    def __init__(self):
        self.recs = {}

    def access(self, v, is_w, ev, engname):
        lst = self.recs.setdefault(v.space, [])
        deps = []
        keep = []
        for r in lst:
            lo, hi, rev, rw, rname = r
            if lo < v.hi and v.lo < hi:
                if is_w or rw:
                    deps.append(rev)
                if is_w and v.lo <= lo and hi <= v.hi:
                    continue
                if (not is_w) and (not rw) and rev[0] is ev[0] and lo == v.lo and hi == v.hi:
                    continue
            keep.append(r)
        keep.append((v.lo, v.hi, ev, is_w, engname))
        self.recs[v.space] = keep
        return deps


class Prog:
    def __init__(self):
        self.tr = Tracker()
        self.engs = {}
        self.dsems = []
        self.dsi = 0

    def next_dsem(self):
        d = self.dsems[self.dsi % len(self.dsems)]
        self.dsi += 1
        return d

    def op(self, engname, fn, reads=(), writes=(), dsem=None):
        eng = self.engs[engname]
        if dsem is not None:
            ev = (dsem.sem, dsem.count + 16)
        else:
            ev = (eng.sem, eng.cnt + 1)
        deps = {}
        for v in reads:
            for (s, val) in self.tr.access(v, False, ev, engname):
                deps[s] = max(deps.get(s, 0), val)
        for v in writes:
            for (s, val) in self.tr.access(v, True, ev, engname):
                deps[s] = max(deps.get(s, 0), val)
        if dsem is not None and dsem.count > 0:
            deps[dsem.sem] = max(deps.get(dsem.sem, 0), dsem.count)
        waits = []
        for s, val in deps.items():
            if s is ev[0] and val >= ev[1]:
                continue
            if s is eng.sem and not eng.selfsync:
                continue
            if eng.seen.get(id(s), 0) >= val:
                continue
            eng.seen[id(s)] = val
            waits.append((s, val))
        if dsem is not None:
            dsem.count += 16
            eng.prog.append((waits, fn, dsem.sem, 16))
        else:
            eng.cnt += 1
            eng.prog.append((waits, fn, eng.sem, 1))
        return ev


def build_program(dbg=None):
    dbg = dbg or {}
    nc = bass.Bass("TRN2", target_bir_lowering=False)
    P = Prog()
    es = ExitStack()

    def din(name, shape, dt=F32):
        return nc.dram_tensor(name, list(shape), dt, kind="ExternalInput").ap()

    xs = din("xs", [HALO + OWN, D])
    pos = din("pos", [1, HALO + OWN], I32)
    memb = din("memb", [256, D])
    w_in = din("w_in", [D, INW])
    w_gate = din("w_gate", [D, 2 * D])
    w_ba = din("w_branch_a", [1024, D])
    w_bb = din("w_branch_b", [2048, D])
    w_out = din("w_out", [D, D])
    w_xq = din("w_xq", [D, 512])
    w_xk = din("w_xk", [D, 512])
    w_xv = din("w_xv", [D, 512])
    w_xo = din("w_xo", [512, D])
    w_up = din("w_up", [D, DFF])
    w_down = din("w_down", [DFF, D])
    w_sp = din("w_spatial", [16, 128, 128])
    gpre = din("gpre", [128, 4 * 32])
    gpost = din("gpost", [3, D])
    bgate = din("bgate", [128, 64])
    lng = din("lng", [128, 16])
    lnb_row = din("lnb_row", [1, 2048])
    bsp_row = din("bsp_row", [1, 2048])
    c_bf = din("c_bf", [128, 13 * 128], BF16)
    c_f32 = din("c_f32", [128, 129])
    out = nc.dram_tensor("out", [OWN, D], F32, kind="ExternalOutput").ap()
    if dbg.get("dump"):
        dbgo = nc.dram_tensor("dbgo", [128, 16384], BF16, kind="ExternalOutput").ap()
        dbgf = nc.dram_tensor("dbgf", [128, 64], F32, kind="ExternalOutput").ap()
    kth = nc.dram_tensor("kth", [3, 8, 128, HALO + OWN], BF16, kind="Internal").ap()
    vh = nc.dram_tensor("vh", [3, HALO + OWN, 1024], BF16, kind="Internal").ap()

    def dv(ap, name):
        return V(ap, "d:" + name, 0, 1)

    POOLB = 206 * 1024
    pool = es.enter_context(nc.sbuf_tensor("pool", [128, POOLB // 2], BF16))
    ps = es.enter_context(nc.psum_tensor("ps", [128, 8 * 512], F32))
    cur = [0]

    def alloc(nbytes):
        off = cur[0]
        cur[0] += (nbytes + 63) // 64 * 64
        assert cur[0] <= POOLB, f"SBUF overflow {cur[0]}"
        return off

    def sb(off, nbytes, dt=BF16, pat=None, parts=128, **kw):
        ap = pool[0:parts, off // 2:(off + nbytes) // 2]
        if dt != BF16:
            ap = ap.bitcast(dt)
        if pat:
            ap = ap.rearrange(pat, **kw)
        return V(ap, "sb", off, off + nbytes)

    def sub(v, ap, lo_b, hi_b):
        return V(ap, v.space, v.lo + lo_b, v.lo + hi_b)

    def bank(b, n=512, dt=F32):
        ap = ps[:, b * 512:b * 512 + (n if dt == F32 else n // 2)]
        if dt != F32:
            ap = ap.bitcast(dt)
        return V(ap, "ps", b * 2048, (b + 1) * 2048)

    o_cbf = alloc(13 * 256)
    cbf = sb(o_cbf, 13 * 256)
    ident = cbf.ap[:, 0:128]
    ones = cbf.ap[:, 128:256]
    ropeP = cbf.ap[:, 256:384]

    def maskap(g, typ):
        i = 3 + g * 3 + typ
        return cbf.ap[:, i * 128:(i + 1) * 128]
    halob = cbf.ap[:, 12 * 128:13 * 128]
    o_cf = alloc(129 * 4)
    cf = sb(o_cf, 129 * 4, F32)
    tril = cf.ap[:, 0:128]
    invf = cf.ap[:, 128:129]
    o_gpre = alloc(128 * 4)
    gpre_s = sb(o_gpre, 128 * 4, F32)
    o_bg = alloc(64 * 4)
    bg_s = sb(o_bg, 64 * 4, F32)
    o_lng = alloc(16 * 4)
    lng_s = sb(o_lng, 16 * 4, F32)
    o_wsT = alloc(16 * 128 * 2)
    wsT = sb(o_wsT, 16 * 128 * 2, BF16, "p (g i) -> p g i", g=16)
    o_R = alloc(16 * 128 * 4)
    Rs = sb(o_R, 16 * 128 * 4, F32, "p (g i) -> p g i", g=16)
    o_kmT = alloc(4 * 256 * 2)
    kmT = sb(o_kmT, 4 * 256 * 2, BF16, "p (h m) -> p h m", h=4)
    o_vm = alloc(2 * 512 * 2)
    vm = sb(o_vm, 2 * 512 * 2, BF16, "p (b c) -> p b c", b=2)
    o_cs = alloc(2 * TT * 4)
    ropeC = sb(o_cs, TT * 4, F32)
    ropeS = sb(o_cs + TT * 4, TT * 4, F32)
    o_small = alloc(64 * 4)
    small = sb(o_small, 64 * 4, F32)
    ring = [sb(alloc(8192), 8192) for _ in range(RING)]
    o_hT = alloc(32 * TT * 2)
    hT = sb(o_hT, 32 * TT * 2, BF16, "p (c t) -> p c t", c=32)
    o_ytok = alloc(NTB * D * 4)
    ytok = [sb(o_ytok + tb * D * 4, D * 4, F32) for tb in range(NTB)]
    o_xt = alloc(D * 4)
    xt = sb(o_xt, D * 4, F32)
    o_hn = alloc(D * 2)
    hn = sb(o_hn, D * 2)
    UBASE = cur[0]
    USIZE = 24 * 1024
    assert UBASE + USIZE <= POOLB, f"SBUF overflow {UBASE + USIZE}"
    Y = o_ytok
    hnb = [hn, sb(UBASE, D * 2)]
    xth = [sb(o_xt, 8192, F32), sb(o_xt + 8192, 8192, F32)]
    xbufs = [xt, ytok[0], ytok[1]]
    nrm_cnt = [0]

    for name, ss in (("pe", False), ("act", True), ("dve", True), ("pool", True), ("sp", True)):
        P.engs[name] = Eng(name, es.enter_context(nc.semaphore("s_" + name)), ss)
    P.dsems = [DSem(es.enter_context(nc.semaphore(f"dq{i}"))) for i in range(24)]
    wsems = [DSem(es.enter_context(nc.semaphore(f"wq{i}"))) for i in range(RING)]

    def dma(q, outv, inv, dsem=None):
        d = dsem or P.next_dsem()
        o_ap, i_ap = outv.ap, inv.ap
        P.op(q, lambda e: e.dma_start(out=o_ap, in_=i_ap), reads=[inv], writes=[outv], dsem=d)

    def act(outv, inv, func, bias=None, scale=None, accum=None, extra_r=()):
        kw = {}
        if bias is not None:
            kw["bias"] = bias
        if scale is not None:
            kw["scale"] = scale
        if accum is not None:
            kw["accum_out"] = accum.ap
        o_ap, i_ap = outv.ap, inv.ap
        P.op("act", lambda e: e.activation(out=o_ap, in_=i_ap, func=func, **kw),
             reads=[inv] + list(extra_r), writes=[outv] + ([accum] if accum is not None else []))

    def tt(outv, av, bv, op, eng="dve"):
        o, a, b = outv.ap, av.ap, bv.ap
        P.op(eng, lambda e: e.tensor_tensor(out=o, in0=a, in1=b, op=op), reads=[av, bv], writes=[outv])

    def ts(outv, av, s1, s2, op0, op1=None, extra_r=(), eng="dve"):
        o, a = outv.ap, av.ap
        if op1 is None:
            P.op(eng, lambda e: e.tensor_scalar(out=o, in0=a, scalar1=s1, scalar2=None, op0=op0),
                 reads=[av] + list(extra_r), writes=[outv])
        else:
            P.op(eng, lambda e: e.tensor_scalar(out=o, in0=a, scalar1=s1, scalar2=s2, op0=op0, op1=op1),
                 reads=[av] + list(extra_r), writes=[outv])

    def stt(outv, av, scal, bv, op0, op1, extra_r=()):
        o, a, b = outv.ap, av.ap, bv.ap
        P.op("dve", lambda e: e.scalar_tensor_tensor(out=o, in0=a, scalar=scal, in1=b, op0=op0, op1=op1),
             reads=[av, bv] + list(extra_r), writes=[outv])

    def cp(outv, inv, eng="dve"):
        if eng == "act":
            return act(outv, inv, AF.Copy)
        o, i = outv.ap, inv.ap
        P.op(eng, lambda e: e.tensor_copy(out=o, in_=i), reads=[inv], writes=[outv])

    def mm(outv, mms, reads):
        def fn(e):
            ins = None
            for (o, l, r, st, sp) in mms:
                ins = e.matmul(o, lhsT=l, rhs=r, start=st, stop=sp, skip_group_check=True)
            return ins
        P.op("pe", fn, reads=reads, writes=[outv])

    def transp(outv, pairs, reads):
        def fn(e):
            ins = None
            for (o, i) in pairs:
                ins = e.transpose(o, i, ident)
            return ins
        P.op("pe", fn, reads=reads + [cbf], writes=[outv])

    wcount = [0]

    def wload(w_ap, k0, nkc, c0, ncols):
        assert nkc * ncols <= 4096
        i = wcount[0] % RING
        wcount[0] += 1
        slot = ring[i]
        ap = slot.ap[:, 0:nkc * ncols].rearrange("p (k n) -> p k n", k=nkc)
        src = w_ap[k0:k0 + nkc * 128, c0:c0 + ncols].rearrange("(k p) n -> p k n", p=128)
        sv = V(ap, "sb", slot.lo, slot.hi)
        dma("pool", sv, V(src, "d:w", 0, 0), dsem=wsems[i])
        return sv

    bank_rr = [0]

    def next_banks(n, pool_list=(0, 1, 2, 3, 4, 5, 6, 7)):
        res = []
        for _ in range(n):
            res.append(pool_list[bank_rr[0] % len(pool_list)])
            bank_rr[0] += 1
        return res

    def projF(lhs_chunks, w_ap, c0, evac, k0=0, ntok=TT):
        nk = len(lhs_chunks)
        bks = next_banks(4)
        bvs = [bank(b) for b in bks]
        for s0 in range(0, nk, 8):
            n = min(8, nk - s0)
            sv = wload(w_ap, k0 + s0 * 128, n, c0, 512)
            for j in range(4):
                mms = []
                rd = [sv]
                for kk in range(n):
                    kc = s0 + kk
                    lv, lap = lhs_chunks[kc]
                    rd.append(lv)
                    mms.append((bvs[j].ap[:, 0:ntok], sv.ap[:, kk, j * 128:(j + 1) * 128], lap[:, 0:ntok], kc == 0, kc == nk - 1))
                mm(bvs[j], mms, rd)
        for j in range(4):
            evac(j, V(bvs[j].ap[:, 0:ntok], "ps", bvs[j].lo, bvs[j].hi))

    def projT(lhs_chunks, w_ap, k0, c0, evac, ntb=NTB):
        nk = len(lhs_chunks)
        bks = next_banks(ntb)
        bvs = [bank(b) for b in bks]
        for s0 in range(0, nk, 8):
            n = min(8, nk - s0)
            sv = wload(w_ap, k0 + s0 * 128, n, c0, 512)
            for tb in range(ntb):
                mms = []
                rd = [sv]
                for kk in range(n):
                    kc = s0 + kk
                    lv, lap = lhs_chunks[kc]
                    rd.append(lv)
                    mms.append((bvs[tb].ap, lap[:, tb * 128:(tb + 1) * 128], sv.ap[:, kk, :], kc == 0, kc == nk - 1))
                mm(bvs[tb], mms, rd)
        for tb in range(ntb):
            evac(tb, bvs[tb])

    hT_chunks = [(hT, hT.ap[:, c, :]) for c in range(32)]

    def sc(i):
        return sub(small, small.ap[:, i:i + 1], i * 4, i * 4 + 4)

    def rstd_from_sum(sumv, outv, n):
        ts(outv, sumv, 1.0 / n, EPS, ALU.mult, ALU.add)
        act(outv, outv, AF.Sqrt)
        o, i = outv.ap, outv.ap
        P.op("dve", lambda e: e.reciprocal(out=o, in_=i), reads=[outv], writes=[outv])

    def normA(xv, ss_i, rs_i):
        hb = hnb[nrm_cnt[0] % 2]
        nrm_cnt[0] += 1
        ssv = sc(ss_i)
        act(hb, xv, AF.Square, accum=ssv)
        rs = sc(rs_i)
        rstd_from_sum(ssv, rs, D)
        return (hb, rs, xv)

    def normB(st, tb, gidx):
        hb, rs, xv = st
        act(hb, xv, AF.Identity, scale=rs.ap, extra_r=[rs])
        for f0 in range(0, 32, 4):
            b = next_banks(1)[0]
            bv = bank(b)
            bt = bv.ap.rearrange("p (c t) -> p c t", c=4)
            mm(bv, [(bt[:, j, :], hb.ap[:, (f0 + j) * 128:(f0 + j + 1) * 128], ident, True, True) for j in range(4)], [hb, cbf])
            g_ap = gpre_s.ap[:, gidx * 32 + f0:gidx * 32 + f0 + 4].unsqueeze(2).to_broadcast([128, 4, 128])
            ov = V(hT.ap[:, f0:f0 + 4, tb * 128:(tb + 1) * 128], "sb", hT.lo, hT.hi)
            tt(ov, V(bt, "ps", bv.lo, bv.hi), V(g_ap, "sb", gpre_s.lo, gpre_s.hi), ALU.mult)

    def preload_x(row_src, ntb=NTB):
        for tb in range(min(ntb, len(xbufs))):
            dma("sp", xbufs[tb], row_src(tb))

    def norm_to_hT_single(row_src, gidx, ntb=NTB):
        for tb in range(ntb):
            dma("sp", xt, row_src(tb))
            st = normA(xt, 48 + (tb % 2), 50 + (tb % 2))
            normB(st, tb, gidx)

    def norm_to_hT(row_src, gidx, ntb=NTB, preloaded=False):
        xvs = []
        for tb in range(ntb):
            xv = xbufs[tb % len(xbufs)]
            if tb < len(xbufs) and not preloaded:
                dma("sp", xv, row_src(tb))
            xvs.append(xv)
        sts = {0: normA(xvs[0], 48, 50)}
        for tb in range(ntb):
            if tb + 1 < ntb:
                sts[tb + 1] = normA(xvs[tb + 1], 48 + ((tb + 1) % 2), 50 + ((tb + 1) % 2))
            normB(sts[tb], tb, gidx)
            if tb + len(xbufs) < ntb:
                dma("sp", xvs[tb], row_src(tb + len(xbufs)))

    def post_residual(src_rows, dst_rows, gi, next_gidx=None):
        dma("sp", gbc, dv(gpost[gi:gi + 1, :].partition_broadcast(128)[:, 0, :], "gpost"))
        rss = {}

        def s1(tb):
            ssv = sc(2 + tb)
            act(hnb[nrm_cnt[0] % 2], ytok[tb], AF.Square, accum=ssv)
            rs = sc(6 + tb)
            rstd_from_sum(ssv, rs, D)
            rss[tb] = rs

        def ld(tb):
            sv = src_rows(tb)
            for h in range(2):
                dma("sp", xth[h], V(sv.ap[:, h * 2048:(h + 1) * 2048], sv.space, sv.lo, sv.hi))

        def s2(tb):
            rs = rss[tb]
            for h in range(2):
                yh = V(ytok[tb].ap[:, h * 2048:(h + 1) * 2048], "sb", ytok[tb].lo + h * 8192, ytok[tb].lo + (h + 1) * 8192)
                gh = V(gbc.ap[:, h * 2048:(h + 1) * 2048], "sb", gbc.lo + h * 8192, gbc.lo + (h + 1) * 8192)
                tt(yh, yh, gh, ALU.mult)
                stt(yh, yh, rs.ap, xth[h], ALU.mult, ALU.add, extra_r=[rs])
            if tb + 1 < NTB:
                ld(tb + 1)
            dma("sp", dst_rows(tb), ytok[tb])

        ld(0)
        s1(0)
        if NTB > 1:
            s1(1)
        if next_gidx is None:
            for tb in range(NTB):
                s2(tb)
                if tb + 2 < NTB:
                    s1(tb + 2)
            return
        sts = {}
        for tb in range(NTB):
            s2(tb)
            if tb >= 1:
                normB(sts[tb - 1], tb - 1, next_gidx)
            hb = hnb[nrm_cnt[0] % 2]
            nrm_cnt[0] += 1
            chains = []
            if tb + 2 < NTB:
                ss1 = sc(2 + tb + 2)
                act(hb, ytok[tb + 2], AF.Square, accum=ss1)
                rss[tb + 2] = sc(6 + tb + 2)
                chains.append((ss1, rss[tb + 2]))
            ssA = sc(10 + tb)
            act(hb, ytok[tb], AF.Square, accum=ssA)
            rsA = sc(14 + tb)
            chains.append((ssA, rsA))
            for (a, b) in chains:
                ts(b, a, 1.0 / D, EPS, ALU.mult, ALU.add)
            for (a, b) in chains:
                act(b, b, AF.Sqrt)
            for (a, b) in chains:
                P.op("dve", lambda e, o=b.ap: e.reciprocal(out=o, in_=o), reads=[b], writes=[b])
            sts[tb] = (hb, rsA, ytok[tb])
        normB(sts[NTB - 1], NTB - 1, next_gidx)

    dma("sp", cbf, dv(c_bf, "c"))
    dma("sp", cf, dv(c_f32, "c"))
    dma("sp", gpre_s, dv(gpre, "c"))
    dma("sp", bg_s, dv(bgate, "c"))
    dma("sp", lng_s, dv(lng, "c"))

    U = [UBASE]

    def ualloc(n):
        off = U[0]
        U[0] += (n + 63) // 64 * 64
        assert U[0] <= POOLB, f"SBUF union overflow {U[0]}"
        return off

    U[0] = Y
    o_ws = ualloc(16 * 128 * 4)
    wsf = sb(o_ws, 16 * 128 * 4, F32, "p (g j) -> p g j", g=16)
    o_wsm = ualloc(16 * 128 * 2)
    wsm = sb(o_wsm, 16 * 128 * 2, BF16, "p (g j) -> p g j", g=16)
    dma("sp", wsf, dv(w_sp.rearrange("g i j -> i g j"), "wsp"))
    tril_b = V(tril.unsqueeze(1).to_broadcast([128, 16, 128]), "sb", cf.lo, cf.hi)
    tt(wsm, wsf, tril_b, ALU.mult)
    for g in range(16):
        b = next_banks(1)[0]
        bv = bank(b, 128, BF16)
        transp(bv, [(bv.ap, wsm.ap[:, g, :])], [wsm])
        cp(V(wsT.ap[:, g, :], "sb", wsT.lo + g * 256, wsT.lo + (g + 1) * 256), bv)
    o_r2 = ualloc(2048 * 4)
    rhs2 = sb(o_r2, 2048 * 4, F32, parts=2)
    o_l2 = ualloc(2048 * 4)
    lhs2 = sb(o_l2, 2048 * 4, F32, parts=2)
    o, = (lhs2.ap,)
    P.op("dve", lambda e: e.memset(lhs2.ap, 1.0), writes=[lhs2])
    dma("sp", V(pool[0:1, o_l2 // 2:(o_l2 + 8192) // 2].bitcast(F32), "sb", lhs2.lo, lhs2.hi), dv(lnb_row, "c"))
    dma("sp", V(pool[1:2, o_r2 // 2:(o_r2 + 8192) // 2].bitcast(F32), "sb", rhs2.lo, rhs2.hi), dv(bsp_row, "c"))
    for q in range(4):
        b = next_banks(1)[0]
        bv = bank(b)
        mm(bv, [(bv.ap[0:1, j * 128:(j + 1) * 128], ones[:, 0:1], wsT.ap[:, q * 4 + j, :], True, True) for j in range(4)],
           [wsT, cbf])
        cp(V(rhs2.ap[0:1, q * 512:(q + 1) * 512], "sb", rhs2.lo, rhs2.hi), V(bv.ap[0:1, :], "ps", bv.lo, bv.hi), eng="act")
    for g in range(16):
        b = next_banks(1)[0]
        bv = bank(b)
        mm(bv, [(bv.ap[:, 0:128], lhs2.ap[:, g * 128:(g + 1) * 128], rhs2.ap[:, g * 128:(g + 1) * 128], True, True)],
           [lhs2, rhs2])
        cp(V(Rs.ap[:, g, :], "sb", Rs.lo + g * 512, Rs.lo + (g + 1) * 512), V(bv.ap[:, 0:128], "ps", bv.lo, bv.hi))

    norm_to_hT(lambda tb: dv(memb[tb * 128:(tb + 1) * 128, :], "memb"), 3, ntb=2)

    def ev_km(j, bv):
        cp(V(kmT.ap[:, j, :], "sb", kmT.lo + j * 512, kmT.lo + (j + 1) * 512), bv, eng="act")
    projF(hT_chunks, w_xk, 0, ev_km, ntok=256)

    def ev_vm(tb, bv):
        cp(V(vm.ap[:, tb, :], "sb", vm.lo + tb * 1024, vm.lo + (tb + 1) * 1024), bv, eng="act")
    projT(hT_chunks, w_xv, 0, 0, ev_vm, ntb=2)

    vgs = [sb(Y + tb * 8192, 8192, F32) for tb in range(NTB)]
    o_uT = Y + NTB * 8192
    uT = sb(o_uT, 16 * TT * 2, BF16, "p (c t) -> p c t", c=16)
    o_vtok = o_uT + 16 * TT * 2
    vtok = sb(o_vtok, NTB * 2048 * 2, BF16, "p (b c) -> p b c", b=NTB)
    assert o_vtok + NTB * 2048 * 2 <= Y + NTB * D * 4
    U[0] = Y
    o_QT = ualloc(24 * TT * 2)
    QT = sb(o_QT, 24 * TT * 2, BF16, "p (c t) -> p c t", c=24)
    kwin = [sb(ualloc((W + TT) * 2), (W + TT) * 2) for (W, _d) in GROUPS]
    vwin = [sb(ualloc((W + TT) * 2), (W + TT) * 2, BF16, "p (b c) -> p b c", c=128) for (W, _d) in GROUPS]
    kraw = [sb(ualloc(TT * 2), TT * 2) for _ in range(4)]
    rtmp = [sb(ualloc(TT * 4), TT * 4, F32) for _ in range(4)]
    pT = [sb(ualloc(TT * 2), TT * 2) for _ in range(4)]
    rden = sb(ualloc(TT * 4), TT * 4, F32)
    vst = [sb(ualloc(512 * 2), 512 * 2) for _ in range(3)]
    assert U[0] <= Y + NTB * D * 4, "ytok-region overflow"
    assert 16 * TT * 2 <= D * 4
    o_ybT = o_xt
    ybT = sb(o_ybT, 16 * TT * 2, BF16, "p (c t) -> p c t", c=16)
    assert 4 * TT * 4 <= D * 2
    sa = sb(o_hn, 4 * TT * 4, F32, "p (c t) -> p c t", c=4)
    U0, U1, U2, U3 = UBASE, UBASE + 8192, UBASE + 16384, UBASE + 20480
    o_yaT = U0
    yaT = sb(U0, 8 * TT * 2, BF16, "p (c t) -> p c t", c=8)
    sbv = sb(U1, 4 * TT * 4, F32, "p (c t) -> p c t", c=4)
    mT = [sb(U2, 4 * TT * 2, BF16, "p (c t) -> p c t", c=4), sb(U3, 4 * TT * 2, BF16, "p (c t) -> p c t", c=4)]
    sgt = sb(U2, TT * 4, F32)
    angf = sb(U1, TT * 4, F32)
    angi = sb(U1 + TT * 4, TT * 4, I32)
    angk = sb(U1 + 2 * TT * 4, TT * 4, F32)
    gbc = sb(U1, D * 4, F32)
    ang_main = (angf, angi, angk)
    qxT = sb(U0, 4 * TT * 2, BF16, "p (c t) -> p c t", c=4)
    oxT = sb(U0 + 4 * TT * 2, 4 * TT * 2, BF16, "p (c t) -> p c t", c=4)
    pX = [sb(U1 + i * TT * 2, TT * 2) for i in range(4)]
    rdenX = sb(U1 + 4 * TT * 2, TT * 4, F32)
    aT = [sb(U0, 8 * TT * 2, BF16, "p (c t) -> p c t", c=8), sb(U1, 8 * TT * 2, BF16, "p (c t) -> p c t", c=8)]
    rl = [sb(U2 + i * TT * 4, TT * 4, F32) for i in range(3)]
    assert U2 + 3 * TT * 4 <= UBASE + USIZE and U1 + D * 4 <= UBASE + USIZE

    rr = {"kraw": 0, "rtmp": 0, "vst": 0, "pT": 0, "rl": 0}

    def rot(lst, key):
        v = lst[rr[key] % len(lst)]
        rr[key] += 1
        return v

    ang_alt = (sb(o_hn, TT * 4, F32), sb(o_hn + TT * 4, TT * 4, I32), sb(o_hn + 2 * TT * 4, TT * 4, F32))

    def rope_tables(tok0, alt=False):
        angf, angi, angk = ang_alt if alt else ang_main
        dma("sp", angi, dv(pos[0:1, tok0:tok0 + TT].partition_broadcast(128)[:, 0, :], "pos"))
        cp(angf, angi)
        ts(angf, angf, invf, None, ALU.mult, extra_r=[cf])
        for (dst, shift) in ((ropeS, 0.0), (ropeC, PI / 2)):
            ts(angk, angf, shift, 1.0 / TWO_PI, ALU.add, ALU.mult)
            cp(angi, angk)
            cp(angk, angi)
            stt(angk, angk, -TWO_PI, angf, ALU.mult, ALU.add)
            ts(angk, angk, shift, None, ALU.add)
            ts(angk, angk, PI, -PI, ALU.min, ALU.max)
            act(dst, angk, AF.Sin)

    def rope_evac(bv, dstv, dst_ap):
        kr = dstv
        act(V(dst_ap, dstv.space, dstv.lo, dstv.hi), bv, AF.Copy)
        b = next_banks(1)[0]
        rb = bank(b)
        mm(rb, [(rb.ap[0:32, 0:TT], ropeP[:, 0:32], dst_ap, True, True)], [kr, cbf])
        t1 = rot(rtmp, "rtmp")
        t2 = rot(rtmp, "rtmp")
        t1v = V(t1.ap[0:32, :], "sb", t1.lo, t1.hi)
        t2v = V(t2.ap[0:32, :], "sb", t2.lo, t2.hi)
        tt(t1v, V(rb.ap[0:32, 0:TT], "ps", rb.lo, rb.hi), V(ropeS.ap[0:32, :], "sb", ropeS.lo, ropeS.hi), ALU.mult)
        d32 = V(dst_ap[0:32, :], dstv.space, dstv.lo, dstv.hi)
        tt(t2v, d32, V(ropeC.ap[0:32, :], "sb", ropeC.lo, ropeC.hi), ALU.mult)
        tt(d32, t1v, t2v, ALU.add)

    def mixer_kv(tok0, glist):
        for g in glist:
            for half in range(2):
                c0 = 3072 + g * 1024 + half * 512

                def ev_k(j, bv, g=g, half=half):
                    kr = rot(kraw, "kraw")
                    rope_evac(bv, kr, kr.ap)
                    dma("sp", dv(kth[g, half * 4 + j, :, tok0:tok0 + TT], f"kth{g}"), kr)
                projF(hT_chunks, w_in, c0, ev_k)
            for half in range(2):
                c0 = 6144 + g * 1024 + half * 512

                def ev_v(tb, bv, g=g, half=half):
                    st = rot(vst, "vst")
                    cp(st, bv, eng="act")
                    dma("sp", dv(vh[g, tok0 + tb * 128:tok0 + (tb + 1) * 128, half * 512:(half + 1) * 512], f"vh{g}"), st)
                projT(hT_chunks, w_in, 0, c0, ev_v)

    def attention(tok0):
        S_BANKS = (0, 1, 2, 3, 4, 5)
        for h in range(8):
            Ob, Db = bank(6), bank(7)
            blocks = []
            for g, (W, dil) in enumerate(GROUPS):
                nkb = W // 128
                kw, vw = kwin[g], vwin[g]
                ntok = W + TT
                kwv = V(kw.ap[:, 0:ntok], "sb", kw.lo, kw.hi)
                dma("sp", kwv, dv(kth[g, h, :, tok0 - W:tok0 + TT], f"kth{g}"))
                vwv = V(vw.ap[:, 0:ntok // 128, :], "sb", vw.lo, vw.hi)
                dma("sp", vwv, dv(vh[g, tok0 - W:tok0 + TT, h * 128:(h + 1) * 128].rearrange("(b p) c -> p b c", p=128), f"vh{g}"))
                for m in range(-nkb, NTB):
                    qlo, qhi = max(m, 0), min(m + nkb, NTB - 1)
                    blocks.append((g, dil, nkb, m, qlo, qhi, kwv, vwv))
            first = [True]
            pend = []

            def emit_pv(item):
                (g, dil, nkb, m, qlo, qhi, kwv, vwv, pt, n) = item
                st = first[0]
                first[0] = False
                mm(Ob, [(Ob.ap[:, qlo * 128:qlo * 128 + n], vwv.ap[:, m + nkb, :], pt.ap[:, 0:n], st, False)], [vwv, pt])
                mm(Db, [(Db.ap[:, qlo * 128:qlo * 128 + n], ones, pt.ap[:, 0:n], st, False)], [pt, cbf])

            for (g, dil, nkb, m, qlo, qhi, kwv, vwv) in blocks:
                n = (qhi - qlo + 1) * 128
                b = next_banks(1, S_BANKS)[0]
                Sb = bank(b)
                mms = [(Sb.ap[:, 0:n], kwv.ap[:, (m + nkb) * 128:(m + nkb + 1) * 128], QT.ap[:, g * 8 + h, qlo * 128:qlo * 128 + n], True, False)]
                for qb in range(qlo, qhi + 1):
                    dl = qb - m
                    typ = 0 if dl == 0 else (1 if dl == nkb else 2)
                    if not (typ == 2 and dil == 1):
                        mms.append((Sb.ap[:, (qb - qlo) * 128:(qb - qlo + 1) * 128], ident, maskap(g, typ), False, False))
                if tok0 + 128 * m < HALO:
                    for qb in range(qlo, qhi + 1):
                        mms.append((Sb.ap[:, (qb - qlo) * 128:(qb - qlo + 1) * 128], ident, halob, False, False))
                mm(Sb, mms, [kwv, QT, cbf])
                pt = rot(pT, "pT")
                act(V(pt.ap[:, 0:n], "sb", pt.lo, pt.hi), V(Sb.ap[:, 0:n], "ps", Sb.lo, Sb.hi), AF.Exp, scale=SCALE)
                pend.append((g, dil, nkb, m, qlo, qhi, kwv, vwv, pt, n))
                if len(pend) > 1:
                    emit_pv(pend.pop(0))
            while pend:
                emit_pv(pend.pop(0))
            P.op("dve", lambda e, o=rden.ap, i=Db.ap[:, 0:TT]: e.reciprocal(out=o, in_=i), reads=[Db], writes=[rden])
            tt(V(yaT.ap[:, h, :], "sb", yaT.lo + h * TT * 2, yaT.lo + (h + 1) * TT * 2), V(Ob.ap[:, 0:TT], "ps", Ob.lo, Ob.hi), rden, ALU.mult)

    def mixer_full(tok0):
        row0 = tok0 - HALO
        for q in range(4):
            def ev_v2(tb, bv, q=q):
                act(V(vgs[tb].ap[:, q * 512:(q + 1) * 512], "sb", vgs[tb].lo + q * 2048, vgs[tb].lo + (q + 1) * 2048), bv,
                    AF.Gelu_apprx_tanh)
            projT(hT_chunks, w_in, 0, 11264 + q * 512, ev_v2)
        for tb in range(NTB):
            s4 = V(small.ap[:, 16 + tb * 4:20 + tb * 4], "sb", small.lo + (16 + tb * 4) * 4, small.lo + (20 + tb * 4) * 4)
            mean = sc(32 + tb)
            act(V(hn.ap[:, 0:2048], 'sb', hn.lo, hn.hi), vgs[tb], AF.Identity, accum=mean)
            ts(mean, mean, -1.0 / 2048, None, ALU.mult)
            ssq = sc(36 + tb)
            act(V(hn.ap[:, 0:2048], 'sb', hn.lo, hn.hi), vgs[tb], AF.Square, accum=ssq)
            var = sc(40 + tb)
            m2 = sc(44 + tb)
            ts(m2, mean, mean.ap, -1.0, ALU.mult, ALU.mult, extra_r=[mean])
            ts(var, ssq, 1.0 / 2048, EPS, ALU.mult, ALU.add)
            act(var, var, AF.Sqrt, bias=m2.ap, extra_r=[m2])
            P.op("dve", lambda e, o=var.ap: e.reciprocal(out=o, in_=o), reads=[var], writes=[var])
            ts(V(vtok.ap[:, tb, :], "sb", vtok.lo + tb * 4096, vtok.lo + (tb + 1) * 4096), vgs[tb], mean.ap, var.ap,
               ALU.add, ALU.mult, extra_r=[mean, var])
        for q in range(4):
            def ev_u(j, bv, q=q):
                c = q * 4 + j
                act(V(uT.ap[:, c, :], "sb", uT.lo + c * TT * 2, uT.lo + (c + 1) * TT * 2), bv, AF.Gelu_apprx_tanh)
            projF(hT_chunks, w_in, 9216 + q * 512, ev_u)
        for cg in range(16):
            b = next_banks(1)[0]
            bv = bank(b)
            mm(bv, [(bv.ap[:, tb * 128:(tb + 1) * 128], vtok.ap[:, tb, cg * 128:(cg + 1) * 128], wsT.ap[:, cg, :], True, True)
                    for tb in range(NTB)], [vtok, wsT])
            b3 = bv.ap[:, 0:TT].rearrange("p (b i) -> p b i", b=NTB)
            r3 = Rs.ap[:, cg, :].unsqueeze(1).to_broadcast([128, NTB, 128])
            s3 = sgt.ap.rearrange("p (b i) -> p b i", b=NTB)
            stt(V(s3, "sb", sgt.lo, sgt.hi), V(b3, "ps", bv.lo, bv.hi), lng_s.ap[:, cg:cg + 1], V(r3, "sb", Rs.lo, Rs.hi),
                ALU.mult, ALU.add, extra_r=[lng_s])
            tt(V(ybT.ap[:, cg, :], "sb", ybT.lo + cg * TT * 2, ybT.lo + (cg + 1) * TT * 2), sgt,
               V(uT.ap[:, cg, :], "sb", uT.lo + cg * TT * 2, uT.lo + (cg + 1) * TT * 2), ALU.mult)
        for g in range(3):
            for half in range(2):
                def ev_q(j, bv, g=g, half=half):
                    c = g * 8 + half * 4 + j
                    rope_evac(bv, V(QT.ap, "sb", QT.lo + c * TT * 2, QT.lo + (c + 1) * TT * 2), QT.ap[:, c, :])
                projF(hT_chunks, w_in, g * 1024 + half * 512, ev_q)
        attention(tok0)
        if dbg.get("dump") and tok0 == HALO:
            dma("sp", dv(dbgf, "dbgf"), small)
            dma("sp", dv(dbgo[:, 0:2048], "dbg"), V(pool[:, o_yaT // 2:o_yaT // 2 + 2048], "sb", yaT.lo, yaT.hi))
            dma("sp", dv(dbgo[:, 2048:6144], "dbg"), V(pool[:, o_ybT // 2:o_ybT // 2 + 4096], "sb", ybT.lo, ybT.hi))
            dma("sp", dv(dbgo[:, 6144:10240], "dbg"), V(pool[:, o_uT // 2:o_uT // 2 + 4096], "sb", uT.lo, uT.hi))
            dma("sp", dv(dbgo[:, 10240:16384], "dbg"), V(pool[:, o_QT // 2:o_QT // 2 + 6144], "sb", QT.lo, QT.hi))
        ya_chunks = [(yaT, yaT.ap[:, c, :]) for c in range(8)]
        yb_chunks = [(ybT, ybT.ap[:, c, :]) for c in range(16)]
        for cg in range(8):
            def ev_ga(j, bv, cg=cg):
                act(V(sa.ap[:, j, :], "sb", sa.lo + j * TT * 4, sa.lo + (j + 1) * TT * 4), bv, AF.Sigmoid,
                    bias=bg_s.ap[:, cg * 4 + j:cg * 4 + j + 1], extra_r=[bg_s])
            projF(hT_chunks, w_gate, cg * 512, ev_ga)

            def ev_gb(j, bv, cg=cg):
                act(V(sbv.ap[:, j, :], "sb", sbv.lo + j * TT * 4, sbv.lo + (j + 1) * TT * 4), bv, AF.Sigmoid,
                    bias=bg_s.ap[:, 32 + cg * 4 + j:32 + cg * 4 + j + 1], extra_r=[bg_s])
            projF(hT_chunks, w_gate, D + cg * 512, ev_gb)

            def ev_a(j, bv):
                v_ = V(sa.ap[:, j, :], "sb", sa.lo + j * TT * 4, sa.lo + (j + 1) * TT * 4)
                tt(v_, bv, v_, ALU.mult)
            projF(ya_chunks, w_ba, cg * 512, ev_a)
            mt = mT[cg % 2]

            def ev_b(j, bv, mt=mt):
                v_ = V(sbv.ap[:, j, :], "sb", sbv.lo + j * TT * 4, sbv.lo + (j + 1) * TT * 4)
                tt(v_, bv, v_, ALU.mult)
                tt(V(mt.ap[:, j, :], "sb", mt.lo + j * TT * 2, mt.lo + (j + 1) * TT * 2),
                   V(sa.ap[:, j, :], "sb", sa.lo + j * TT * 4, sa.lo + (j + 1) * TT * 4), v_, ALU.add)
            projF(yb_chunks, w_bb, cg * 512, ev_b)
            m_chunks = [(mt, mt.ap[:, c, :]) for c in range(4)]
            for ocg in range(8):
                def ev_o(tb, bv, ocg=ocg, cg=cg):
                    yv = V(ytok[tb].ap[:, ocg * 512:(ocg + 1) * 512], "sb", ytok[tb].lo + ocg * 2048, ytok[tb].lo + (ocg + 1) * 2048)
                    if cg == 0:
                        cp(yv, bv, eng="act")
                    else:
                        tt(yv, bv, yv, ALU.add)
                projT(m_chunks, w_out, cg * 512, ocg * 512, ev_o)
        post_residual(lambda tb: dv(xs[tok0 + tb * 128:tok0 + (tb + 1) * 128, :], "xs"),
                      lambda tb: dv(out[row0 + tb * 128:row0 + (tb + 1) * 128, :], f"out{row0 + tb * 128}"), 0, next_gidx=1)

    def xa_phase(row0):
        rows = lambda tb: dv(out[row0 + tb * 128:row0 + (tb + 1) * 128, :], f"out{row0 + tb * 128}")

        def ev_q(j, bv):
            cp(V(qxT.ap[:, j, :], "sb", qxT.lo + j * TT * 2, qxT.lo + (j + 1) * TT * 2), bv, eng="act")
        projF(hT_chunks, w_xq, 0, ev_q)
        for h in range(4):
            Ob, Db = bank(6), bank(7)
            pts = []
            for mb in range(2):
                b = next_banks(1, (0, 1, 2, 3, 4, 5))[0]
                Sb = bank(b)
                mm(Sb, [(Sb.ap[:, 0:TT], kmT.ap[:, h, mb * 128:(mb + 1) * 128], qxT.ap[:, h, :], True, True)], [kmT, qxT])
                pt = rot(pX, "pT")
                act(pt, V(Sb.ap[:, 0:TT], "ps", Sb.lo, Sb.hi), AF.Exp, scale=SCALE)
                pts.append(pt)
            for mb in range(2):
                mm(Ob, [(Ob.ap[:, 0:TT], vm.ap[:, mb, h * 128:(h + 1) * 128], pts[mb].ap, mb == 0, mb == 1)], [vm, pts[mb]])
                mm(Db, [(Db.ap[:, 0:TT], ones, pts[mb].ap, mb == 0, mb == 1)], [pts[mb], cbf])
            P.op("dve", lambda e, o=rdenX.ap, i=Db.ap[:, 0:TT]: e.reciprocal(out=o, in_=i), reads=[Db], writes=[rdenX])
            tt(V(oxT.ap[:, h, :], "sb", oxT.lo + h * TT * 2, oxT.lo + (h + 1) * TT * 2), V(Ob.ap[:, 0:TT], "ps", Ob.lo, Ob.hi), rdenX, ALU.mult)
        ox_chunks = [(oxT, oxT.ap[:, c, :]) for c in range(4)]
        for ocg in range(8):
            def ev_o(tb, bv, ocg=ocg):
                cp(V(ytok[tb].ap[:, ocg * 512:(ocg + 1) * 512], "sb", ytok[tb].lo + ocg * 2048, ytok[tb].lo + (ocg + 1) * 2048), bv, eng="act")
            projT(ox_chunks, w_xo, 0, ocg * 512, ev_o)
        post_residual(rows, rows, 1, next_gidx=2)

    def mlp_phase(row0, early_next=None):
        rows = lambda tb: dv(out[row0 + tb * 128:row0 + (tb + 1) * 128, :], f"out{row0 + tb * 128}")
        for sec in range(DFF // SEC):
            at = aT[sec % 2]
            for half in range(SEC // 512):
                def ev_up(j, bv, half=half, at=at):
                    r = rot(rl, "rl")
                    act(r, bv, AF.Relu)
                    c = half * 4 + j
                    tt(V(at.ap[:, c, :], "sb", at.lo + c * TT * 2, at.lo + (c + 1) * TT * 2), r, r, ALU.mult)
                projF(hT_chunks, w_up, sec * SEC + half * 512, ev_up)
            a_chunks = [(at, at.ap[:, c, :]) for c in range(SEC // 128)]
            if sec == DFF // SEC - 1 and early_next is not None:
                early_next()
            for ocg in range(8):
                def ev_dn(tb, bv, ocg=ocg, sec=sec):
                    yv = V(ytok[tb].ap[:, ocg * 512:(ocg + 1) * 512], "sb", ytok[tb].lo + ocg * 2048, ytok[tb].lo + (ocg + 1) * 2048)
                    if sec == 0:
                        cp(yv, bv, eng="act")
                    else:
                        tt(yv, bv, yv, ALU.add)
                projT(a_chunks, w_down, sec * SEC, ocg * 512, ev_dn)
        post_residual(rows, rows, 2)

    state = {}
    for t in range(NHT + NT):
        tok0 = t * TT
        is_halo = t < NHT
        if is_halo:
            glist = [g for g, (W, dil) in enumerate(GROUPS) if HALO - (tok0 + TT) < W]
        else:
            glist = [0, 1, 2]
        xrows = lambda tb, tok0=tok0: dv(xs[tok0 + tb * 128:tok0 + (tb + 1) * 128, :], "xs")
        if not state.get("prenormed"):
            rope_tables(tok0)
            norm_to_hT(xrows, 0, preloaded=state.get("preloaded", False))
        state["prenormed"] = False
        state["preloaded"] = False
        if is_halo and t + 1 < NHT + NT:
            ntok0 = (t + 1) * TT
            preload_x(lambda tb, ntok0=ntok0: dv(xs[ntok0 + tb * 128:ntok0 + (tb + 1) * 128, :], "xs"))
            state["preloaded"] = True
        mixer_kv(tok0, glist)
        if not is_halo:
            mixer_full(tok0)
            if dbg.get("stop") == "mixer":
                break
            xa_phase(tok0 - HALO)
            if dbg.get("stop") == "xa":
                break
            early = None
            if t + 1 < NHT + NT and (not dbg.get("ntiles") or (t - NHT + 1) < dbg["ntiles"]):
                ntok0 = (t + 1) * TT

                def early(ntok0=ntok0):
                    rope_tables(ntok0, alt=True)
                    norm_to_hT_single(lambda tb: dv(xs[ntok0 + tb * 128:ntok0 + (tb + 1) * 128, :], "xs"), 0)
                    state["prenormed"] = True
            mlp_phase(tok0 - HALO, early)
            if dbg.get("ntiles") and t - NHT + 1 >= dbg["ntiles"]:
                break

    final_waits = [(d.sem, d.count) for d in P.dsems + wsems if d.count > 0]

    with nc.Block() as block:
        def replay(e, eng):
            for waits, fn, sem, inc in eng.prog:
                for (s, v) in waits:
                    e.wait_ge(s, v)
                ins = fn(e)
                ins.then_inc(sem, inc)

        @block.tensor
        def _(e):
            replay(e, P.engs["pe"])

        @block.scalar
        def _(e):
            replay(e, P.engs["act"])

        @block.vector
        def _(e):
            replay(e, P.engs["dve"])

        @block.gpsimd
        def _(e):
            replay(e, P.engs["pool"])

        @block.sync
        def _(e):
            replay(e, P.engs["sp"])
            for (s, v) in final_waits:
                e.wait_ge(s, v)
            for nm in ("pe", "act", "dve", "pool"):
                en = P.engs[nm]
                if en.cnt:
                    e.wait_ge(en.sem, en.cnt)
    es.close()
    return nc


def _consts(core):
    bf = ml_dtypes.bfloat16
    c = np.zeros((128, 13, 128), np.float32)
    c[:, 0] = np.eye(128)
    c[:, 1] = 1.0
    for i in range(16):
        c[i + 16, 2, i] = -1.0
        c[i, 2, i + 16] = 1.0
    k = np.arange(128)[:, None]
    q = np.arange(128)[None, :]
    for g, (W, dil) in enumerate(GROUPS):
        res = ((q - k) % dil) == 0
        c[:, 3 + g * 3 + 0] = np.where(res & (q - k >= 0), 0.0, NEGM)
        c[:, 3 + g * 3 + 1] = np.where(res & (q - k <= 0), 0.0, NEGM)
        c[:, 3 + g * 3 + 2] = np.where(res, 0.0, NEGM)
    c[:, 12] = NEGM if core % 2 == 0 else 0.0
    cf = np.zeros((128, 129), np.float32)
    cf[:, :128] = np.tril(np.ones((128, 128), np.float32))
    inv = (np.float32(500000.0) ** (-np.arange(0, 32, 2, dtype=np.float32) / np.float32(32))).astype(np.float32)
    cf[0:16, 128] = inv
    cf[16:32, 128] = inv
    return c.reshape(128, 13 * 128).astype(bf), cf


_NC_CACHE = {}


def kernel(x, mem, positions, mix_pre_g, w_in, sgu_ln_g, sgu_ln_b, w_spatial, b_spatial,
           w_branch_a, w_branch_b, w_gate, b_gate, w_out, mix_post_g, xa_pre_g, mem_norm_g,
           w_xq, w_xk, w_xv, w_xo, xa_post_g, mlp_pre_g, w_up, w_down, mlp_post_g):
    f = lambda a: np.ascontiguousarray(np.asarray(a))
    x = np.asarray(x)
    positions = np.asarray(positions)
    col = lambda g: np.asarray(g)[0].reshape(-1, 128).T
    gpre = f(np.concatenate([col(mix_pre_g), col(xa_pre_g), col(mlp_pre_g), col(mem_norm_g)], axis=1).astype(np.float32))
    gpost = f(np.stack([np.asarray(mix_post_g)[0], np.asarray(xa_post_g)[0], np.asarray(mlp_post_g)[0]]).astype(np.float32))
    shared = {
        "w_in": f(np.asarray(w_in)[0]), "w_gate": f(np.asarray(w_gate)[0]), "w_branch_a": f(np.asarray(w_branch_a)[0]),
        "w_branch_b": f(np.asarray(w_branch_b)[0]), "w_out": f(np.asarray(w_out)[0]), "w_xq": f(np.asarray(w_xq)[0]),
        "w_xk": f(np.asarray(w_xk)[0]), "w_xv": f(np.asarray(w_xv)[0]), "w_xo": f(np.asarray(w_xo)[0]),
        "w_up": f(np.asarray(w_up)[0]), "w_down": f(np.asarray(w_down)[0]), "w_spatial": f(np.asarray(w_spatial)[0]),
        "gpre": gpre, "gpost": gpost, "bgate": f(col(b_gate)), "lng": f(col(sgu_ln_g)),
        "lnb_row": f(np.asarray(sgu_ln_b)[0].reshape(1, 2048)), "bsp_row": f(np.asarray(b_spatial)[0].reshape(1, 2048)),
    }
    in_maps = []
    for c in range(8):
        b, half = c // 2, c % 2
        own = x[b, half * OWN:(half + 1) * OWN]
        if half == 1:
            halo = x[b, 0:HALO]
            ph = positions[b, 0:HALO]
        else:
            halo = np.zeros((HALO, D), np.float32)
            ph = np.zeros((HALO,), np.int32)
        cb, cf = _consts(c)
        m = dict(shared)
        m["xs"] = f(np.concatenate([halo, own], axis=0))
        m["pos"] = f(np.concatenate([ph, positions[b, half * OWN:(half + 1) * OWN]]).reshape(1, -1).astype(np.int32))
        m["memb"] = f(np.asarray(mem)[b])
        m["c_bf"] = cb
        m["c_f32"] = cf
        in_maps.append(m)
    dbg = _NC_CACHE.get("dbg")
    if "nc" not in _NC_CACHE:
        _NC_CACHE["nc"] = build_program(dbg)
    nc = _NC_CACHE["nc"]
    ncores = (dbg or {}).get("ncores", 8)
    res = run_bass_kernel_spmd(nc, in_maps[:ncores], core_ids=list(range(ncores)))
    if ncores < 8:
        _NC_CACHE["last"] = res.results[0]
        return res.results[0]["out"]
    outp = np.empty((4, 4096, D), np.float32)
    for c in range(8):
        b, half = c // 2, c % 2
        outp[b, half * OWN:(half + 1) * OWN] = res.results[c]["out"]
    return outp
```

```python
import numpy as np
import ml_dtypes
from contextlib import ExitStack
import concourse.bass as bass
import concourse.mybir as mybir
from concourse.bass_utils import run_bass_kernel_spmd

F32 = mybir.dt.float32
BF16 = mybir.dt.bfloat16
I32 = mybir.dt.int32
AF = mybir.ActivationFunctionType
ALU = mybir.AluOpType

D = 4096
TT = 512
NTB = TT // 128
OWN = 2048
HALO = 2048
NT = OWN // TT
NHT = HALO // TT
GROUPS = ((128, 1), (512, 4), (2048, 16))
INW = 13312
DFF = 16384
SEC = 1024
EPS = 1e-6
NEGM = -30000.0
SCALE = 128 ** -0.5
RING = 4
POOL_ENG = "pool"
PI = float(np.pi)
TWO_PI = float(2 * np.pi)


class Eng:
    def __init__(self, name, sem, selfsync):
        self.name, self.sem, self.cnt, self.prog, self.seen, self.selfsync = name, sem, 0, [], {}, selfsync


class DSem:
    def __init__(self, sem):
        self.sem, self.count = sem, 0


class V:
    def __init__(self, ap, space, lo, hi):
        self.ap, self.space, self.lo, self.hi = ap, space, lo, hi


class Tracker:
    def __init__(self):
        self.recs = {}

    def access(self, v, is_w, ev, engname):
        lst = self.recs.setdefault(v.space, [])
        deps = []
        keep = []
        for r in lst:
            lo, hi, rev, rw, rname = r
            if lo < v.hi and v.lo < hi:
                if is_w or rw:
                    deps.append(rev)
                if is_w and v.lo <= lo and hi <= v.hi:
                    continue
                if (not is_w) and (not rw) and rev[0] is ev[0] and lo == v.lo and hi == v.hi:
                    continue
            keep.append(r)
        keep.append((v.lo, v.hi, ev, is_w, engname))
        self.recs[v.space] = keep
        return deps


class Prog:
    def __init__(self):
        self.tr = Tracker()
        self.engs = {}
        self.dsems = []
        self.dsi = 0

    def next_dsem(self):
        d = self.dsems[self.dsi % len(self.dsems)]
        self.dsi += 1
        return d

    def op(self, engname, fn, reads=(), writes=(), dsem=None):
        eng = self.engs[engname]
        if dsem is not None:
            ev = (dsem.sem, dsem.count + 16)
        else:
            ev = (eng.sem, eng.cnt + 1)
        deps = {}
        for v in reads:
            for (s, val) in self.tr.access(v, False, ev, engname):
                deps[s] = max(deps.get(s, 0), val)
        for v in writes:
            for (s, val) in self.tr.access(v, True, ev, engname):
                deps[s] = max(deps.get(s, 0), val)
        if dsem is not None and dsem.count > 0:
            deps[dsem.sem] = max(deps.get(dsem.sem, 0), dsem.count)
        waits = []
        for s, val in deps.items():
            if s is ev[0] and val >= ev[1]:
                continue
            if s is eng.sem and not eng.selfsync:
                continue
            if eng.seen.get(id(s), 0) >= val:
                continue
            eng.seen[id(s)] = val
            waits.append((s, val))
        if dsem is not None:
            dsem.count += 16
            eng.prog.append((waits, fn, dsem.sem, 16))
        else:
            eng.cnt += 1
            eng.prog.append((waits, fn, eng.sem, 1))
        return ev


def build_program(dbg=None):
    dbg = dbg or {}
    nc = bass.Bass("TRN2", target_bir_lowering=False)
    P = Prog()
    es = ExitStack()

    def din(name, shape, dt=F32):
        return nc.dram_tensor(name, list(shape), dt, kind="ExternalInput").ap()

    xs = din("xs", [HALO + OWN, D])
    pos = din("pos", [1, HALO + OWN], I32)
    memb = din("memb", [256, D])
    w_in = din("w_in", [D, INW])
    w_gate = din("w_gate", [D, 2 * D])
    w_ba = din("w_branch_a", [1024, D])
    w_bb = din("w_branch_b", [2048, D])
    w_out = din("w_out", [D, D])
    w_xq = din("w_xq", [D, 512])
    w_xk = din("w_xk", [D, 512])
    w_xv = din("w_xv", [D, 512])
    w_xo = din("w_xo", [512, D])
    w_up = din("w_up", [D, DFF])
    w_down = din("w_down", [DFF, D])
    w_sp = din("w_spatial", [16, 128, 128])
    gpre = din("gpre", [128, 4 * 32])
    gpost = din("gpost", [3, D])
    bgate = din("bgate", [128, 64])
    lng = din("lng", [128, 16])
    lnb_row = din("lnb_row", [1, 2048])
    bsp_row = din("bsp_row", [1, 2048])
    c_bf = din("c_bf", [128, 13 * 128], BF16)
    c_f32 = din("c_f32", [128, 129])
    out = nc.dram_tensor("out", [OWN, D], F32, kind="ExternalOutput").ap()
    if dbg.get("dump"):
        dbgo = nc.dram_tensor("dbgo", [128, 16384], BF16, kind="ExternalOutput").ap()
        dbgf = nc.dram_tensor("dbgf", [128, 64], F32, kind="ExternalOutput").ap()
    kth = nc.dram_tensor("kth", [3, 8, 128, HALO + OWN], BF16, kind="Internal").ap()
    vh = nc.dram_tensor("vh", [3, HALO + OWN, 1024], BF16, kind="Internal").ap()

    def dv(ap, name):
        return V(ap, "d:" + name, 0, 1)

    POOLB = 206 * 1024
    pool = es.enter_context(nc.sbuf_tensor("pool", [128, POOLB // 2], BF16))
    ps = es.enter_context(nc.psum_tensor("ps", [128, 8 * 512], F32))
    cur = [0]

    def alloc(nbytes):
        off = cur[0]
        cur[0] += (nbytes + 63) // 64 * 64
        assert cur[0] <= POOLB, f"SBUF overflow {cur[0]}"
        return off

    def sb(off, nbytes, dt=BF16, pat=None, parts=128, **kw):
        ap = pool[0:parts, off // 2:(off + nbytes) // 2]
        if dt != BF16:
            ap = ap.bitcast(dt)
        if pat:
            ap = ap.rearrange(pat, **kw)
        return V(ap, "sb", off, off + nbytes)

    def sub(v, ap, lo_b, hi_b):
        return V(ap, v.space, v.lo + lo_b, v.lo + hi_b)

    def bank(b, n=512, dt=F32):
        ap = ps[:, b * 512:b * 512 + (n if dt == F32 else n // 2)]
        if dt != F32:
            ap = ap.bitcast(dt)
        return V(ap, "ps", b * 2048, (b + 1) * 2048)

    o_cbf = alloc(13 * 256)
    cbf = sb(o_cbf, 13 * 256)
    ident = cbf.ap[:, 0:128]
    ones = cbf.ap[:, 128:256]
    ropeP = cbf.ap[:, 256:384]

    def maskap(g, typ):
        i = 3 + g * 3 + typ
        return cbf.ap[:, i * 128:(i + 1) * 128]
    halob = cbf.ap[:, 12 * 128:13 * 128]
    o_cf = alloc(129 * 4)
    cf = sb(o_cf, 129 * 4, F32)
    tril = cf.ap[:, 0:128]
    invf = cf.ap[:, 128:129]
    o_gpre = alloc(128 * 4)
    gpre_s = sb(o_gpre, 128 * 4, F32)
    o_bg = alloc(64 * 4)
    bg_s = sb(o_bg, 64 * 4, F32)
    o_lng = alloc(16 * 4)
    lng_s = sb(o_lng, 16 * 4, F32)
    o_wsT = alloc(16 * 128 * 2)
    wsT = sb(o_wsT, 16 * 128 * 2, BF16, "p (g i) -> p g i", g=16)
    o_R = alloc(16 * 128 * 4)
    Rs = sb(o_R, 16 * 128 * 4, F32, "p (g i) -> p g i", g=16)
    o_kmT = alloc(4 * 256 * 2)
    kmT = sb(o_kmT, 4 * 256 * 2, BF16, "p (h m) -> p h m", h=4)
    o_vm = alloc(2 * 512 * 2)
    vm = sb(o_vm, 2 * 512 * 2, BF16, "p (b c) -> p b c", b=2)
    o_cs = alloc(2 * TT * 4)
    ropeC = sb(o_cs, TT * 4, F32)
    ropeS = sb(o_cs + TT * 4, TT * 4, F32)
    o_small = alloc(64 * 4)
    small = sb(o_small, 64 * 4, F32)
    ring = [sb(alloc(8192), 8192) for _ in range(RING)]
    o_hT = alloc(32 * TT * 2)
    hT = sb(o_hT, 32 * TT * 2, BF16, "p (c t) -> p c t", c=32)
    o_ytok = alloc(NTB * D * 4)
    ytok = [sb(o_ytok + tb * D * 4, D * 4, F32) for tb in range(NTB)]
    o_xt = alloc(D * 4)
    xt = sb(o_xt, D * 4, F32)
    o_hn = alloc(D * 2)
    hn = sb(o_hn, D * 2)
    UBASE = cur[0]
    USIZE = 24 * 1024
    assert UBASE + USIZE <= POOLB, f"SBUF overflow {UBASE + USIZE}"
    Y = o_ytok
    hnb = [hn, sb(UBASE, D * 2)]
    xth = [sb(o_xt, 8192, F32), sb(o_xt + 8192, 8192, F32)]
    xbufs = [xt, ytok[0], ytok[1]]
    nrm_cnt = [0]

    for name, ss in (("pe", False), ("act", True), ("dve", True), ("pool", True), ("sp", True)):
        P.engs[name] = Eng(name, es.enter_context(nc.semaphore("s_" + name)), ss)
    P.dsems = [DSem(es.enter_context(nc.semaphore(f"dq{i}"))) for i in range(24)]
    wsems = [DSem(es.enter_context(nc.semaphore(f"wq{i}"))) for i in range(RING)]

    def dma(q, outv, inv, dsem=None):
        d = dsem or P.next_dsem()
        o_ap, i_ap = outv.ap, inv.ap
        P.op(q, lambda e: e.dma_start(out=o_ap, in_=i_ap), reads=[inv], writes=[outv], dsem=d)

    def act(outv, inv, func, bias=None, scale=None, accum=None, extra_r=()):
        kw = {}
        if bias is not None:
            kw["bias"] = bias
        if scale is not None:
            kw["scale"] = scale
        if accum is not None:
            kw["accum_out"] = accum.ap
        o_ap, i_ap = outv.ap, inv.ap
        P.op("act", lambda e: e.activation(out=o_ap, in_=i_ap, func=func, **kw),
             reads=[inv] + list(extra_r), writes=[outv] + ([accum] if accum is not None else []))

    def tt(outv, av, bv, op, eng="dve"):
        o, a, b = outv.ap, av.ap, bv.ap
        P.op(eng, lambda e: e.tensor_tensor(out=o, in0=a, in1=b, op=op), reads=[av, bv], writes=[outv])

    def ts(outv, av, s1, s2, op0, op1=None, extra_r=(), eng="dve"):
        o, a = outv.ap, av.ap
        if op1 is None:
            P.op(eng, lambda e: e.tensor_scalar(out=o, in0=a, scalar1=s1, scalar2=None, op0=op0),
                 reads=[av] + list(extra_r), writes=[outv])
        else:
            P.op(eng, lambda e: e.tensor_scalar(out=o, in0=a, scalar1=s1, scalar2=s2, op0=op0, op1=op1),
                 reads=[av] + list(extra_r), writes=[outv])

    def stt(outv, av, scal, bv, op0, op1, extra_r=()):
        o, a, b = outv.ap, av.ap, bv.ap
        P.op("dve", lambda e: e.scalar_tensor_tensor(out=o, in0=a, scalar=scal, in1=b, op0=op0, op1=op1),
             reads=[av, bv] + list(extra_r), writes=[outv])

    def cp(outv, inv, eng="dve"):
        if eng == "act":
            return act(outv, inv, AF.Copy)
        o, i = outv.ap, inv.ap
        P.op(eng, lambda e: e.tensor_copy(out=o, in_=i), reads=[inv], writes=[outv])

    def mm(outv, mms, reads):
        def fn(e):
            ins = None
            for (o, l, r, st, sp) in mms:
                ins = e.matmul(o, lhsT=l, rhs=r, start=st, stop=sp, skip_group_check=True)
            return ins
        P.op("pe", fn, reads=reads, writes=[outv])

    def transp(outv, pairs, reads):
        def fn(e):
            ins = None
            for (o, i) in pairs:
                ins = e.transpose(o, i, ident)
            return ins
        P.op("pe", fn, reads=reads + [cbf], writes=[outv])

    wcount = [0]

    def wload(w_ap, k0, nkc, c0, ncols):
        assert nkc * ncols <= 4096
        i = wcount[0] % RING
        wcount[0] += 1
        slot = ring[i]
        ap = slot.ap[:, 0:nkc * ncols].rearrange("p (k n) -> p k n", k=nkc)
        src = w_ap[k0:k0 + nkc * 128, c0:c0 + ncols].rearrange("(k p) n -> p k n", p=128)
        sv = V(ap, "sb", slot.lo, slot.hi)
        dma("pool", sv, V(src, "d:w", 0, 0), dsem=wsems[i])
        return sv

    bank_rr = [0]

    def next_banks(n, pool_list=(0, 1, 2, 3, 4, 5, 6, 7)):
        res = []
        for _ in range(n):
            res.append(pool_list[bank_rr[0] % len(pool_list)])
            bank_rr[0] += 1
        return res

    def projF(lhs_chunks, w_ap, c0, evac, k0=0, ntok=TT):
        nk = len(lhs_chunks)
        bks = next_banks(4)
        bvs = [bank(b) for b in bks]
        for s0 in range(0, nk, 8):
            n = min(8, nk - s0)
            sv = wload(w_ap, k0 + s0 * 128, n, c0, 512)
            for j in range(4):
                mms = []
                rd = [sv]
                for kk in range(n):
                    kc = s0 + kk
                    lv, lap = lhs_chunks[kc]
                    rd.append(lv)
                    mms.append((bvs[j].ap[:, 0:ntok], sv.ap[:, kk, j * 128:(j + 1) * 128], lap[:, 0:ntok], kc == 0, kc == nk - 1))
                mm(bvs[j], mms, rd)
        for j in range(4):
            evac(j, V(bvs[j].ap[:, 0:ntok], "ps", bvs[j].lo, bvs[j].hi))

    def projT(lhs_chunks, w_ap, k0, c0, evac, ntb=NTB):
        nk = len(lhs_chunks)
        bks = next_banks(ntb)
        bvs = [bank(b) for b in bks]
        for s0 in range(0, nk, 8):
            n = min(8, nk - s0)
            sv = wload(w_ap, k0 + s0 * 128, n, c0, 512)
            for tb in range(ntb):
                mms = []
                rd = [sv]
                for kk in range(n):
                    kc = s0 + kk
                    lv, lap = lhs_chunks[kc]
                    rd.append(lv)
                    mms.append((bvs[tb].ap, lap[:, tb * 128:(tb + 1) * 128], sv.ap[:, kk, :], kc == 0, kc == nk - 1))
                mm(bvs[tb], mms, rd)
        for tb in range(ntb):
            evac(tb, bvs[tb])

    hT_chunks = [(hT, hT.ap[:, c, :]) for c in range(32)]

    def sc(i):
        return sub(small, small.ap[:, i:i + 1], i * 4, i * 4 + 4)

    def rstd_from_sum(sumv, outv, n):
        ts(outv, sumv, 1.0 / n, EPS, ALU.mult, ALU.add)
        act(outv, outv, AF.Sqrt)
        o, i = outv.ap, outv.ap
        P.op("dve", lambda e: e.reciprocal(out=o, in_=i), reads=[outv], writes=[outv])

    def normA(xv, ss_i, rs_i):
        hb = hnb[nrm_cnt[0] % 2]
        nrm_cnt[0] += 1
        ssv = sc(ss_i)
        act(hb, xv, AF.Square, accum=ssv)
        rs = sc(rs_i)
        rstd_from_sum(ssv, rs, D)
        return (hb, rs, xv)

    def normB(st, tb, gidx):
        normB_id(st)
        normB_tr(st, tb, gidx)

    def normB_id(st):
        hb, rs, xv = st
        act(hb, xv, AF.Identity, scale=rs.ap, extra_r=[rs])

    def normB_tr(st, tb, gidx):
        hb, rs, xv = st
        for f0 in range(0, 32, 4):
            b = next_banks(1)[0]
            bv = bank(b)
            bt = bv.ap.rearrange("p (c t) -> p c t", c=4)
            mm(bv, [(bt[:, j, :], hb.ap[:, (f0 + j) * 128:(f0 + j + 1) * 128], ident, True, True) for j in range(4)], [hb, cbf])
            g_ap = gpre_s.ap[:, gidx * 32 + f0:gidx * 32 + f0 + 4].unsqueeze(2).to_broadcast([128, 4, 128])
            ov = V(hT.ap[:, f0:f0 + 4, tb * 128:(tb + 1) * 128], "sb", hT.lo, hT.hi)
            tt(ov, V(bt, "ps", bv.lo, bv.hi), V(g_ap, "sb", gpre_s.lo, gpre_s.hi), ALU.mult)

    def preload_x(row_src, ntb=NTB):
        for tb in range(min(ntb, len(xbufs))):
            dma("sp", xbufs[tb], row_src(tb))

    def norm_to_hT_single(row_src, gidx, ntb=NTB):
        for tb in range(ntb):
            dma("sp", xt, row_src(tb))
            st = normA(xt, 48 + (tb % 2), 50 + (tb % 2))
            normB(st, tb, gidx)

    def norm_to_hT(row_src, gidx, ntb=NTB, preloaded=False):
        xvs = []
        for tb in range(ntb):
            xv = xbufs[tb % len(xbufs)]
            if tb < len(xbufs) and not preloaded:
                dma("sp", xv, row_src(tb))
            xvs.append(xv)
        sts = {0: normA(xvs[0], 48, 50)}
        for tb in range(ntb):
            if tb + 1 < ntb:
                sts[tb + 1] = normA(xvs[tb + 1], 48 + ((tb + 1) % 2), 50 + ((tb + 1) % 2))
            normB(sts[tb], tb, gidx)
            if tb + len(xbufs) < ntb:
                dma("sp", xvs[tb], row_src(tb + len(xbufs)))

    def post_residual(src_rows, dst_rows, gi, next_gidx=None):
        dma("sp", gbc, dv(gpost[gi:gi + 1, :].partition_broadcast(128)[:, 0, :], "gpost"))
        rss = {}

        def s1(tb):
            ssv = sc(2 + tb)
            act(hnb[nrm_cnt[0] % 2], ytok[tb], AF.Square, accum=ssv)
            rs = sc(6 + tb)
            rstd_from_sum(ssv, rs, D)
            rss[tb] = rs

        def ld(tb):
            sv = src_rows(tb)
            for h in range(2):
                dma("sp", xth[h], V(sv.ap[:, h * 2048:(h + 1) * 2048], sv.space, sv.lo, sv.hi))

        def s2(tb):
            rs = rss[tb]
            for h in range(2):
                yh = V(ytok[tb].ap[:, h * 2048:(h + 1) * 2048], "sb", ytok[tb].lo + h * 8192, ytok[tb].lo + (h + 1) * 8192)
                gh = V(gbc.ap[:, h * 2048:(h + 1) * 2048], "sb", gbc.lo + h * 8192, gbc.lo + (h + 1) * 8192)
                tt(yh, yh, gh, ALU.mult)
                stt(yh, yh, rs.ap, xth[h], ALU.mult, ALU.add, extra_r=[rs])
            if tb + 1 < NTB:
                ld(tb + 1)
            dma("sp", dst_rows(tb), ytok[tb])

        ld(0)
        s1(0)
        if NTB > 1:
            s1(1)
        if next_gidx is None:
            for tb in range(NTB):
                s2(tb)
                if tb + 2 < NTB:
                    s1(tb + 2)
            return
        sts = {}
        for tb in range(NTB):
            s2(tb)
            if tb >= 1:
                normB(sts[tb - 1], tb - 1, next_gidx)
            hb = hnb[nrm_cnt[0] % 2]
            nrm_cnt[0] += 1
            chains = []
            if tb + 2 < NTB:
                ss1 = sc(2 + tb + 2)
                act(hb, ytok[tb + 2], AF.Square, accum=ss1)
                rss[tb + 2] = sc(6 + tb + 2)
                chains.append((ss1, rss[tb + 2]))
            ssA = sc(10 + tb)
            act(hb, ytok[tb], AF.Square, accum=ssA)
            rsA = sc(14 + tb)
            chains.append((ssA, rsA))
            for (a, b) in chains:
                ts(b, a, 1.0 / D, EPS, ALU.mult, ALU.add)
            for (a, b) in chains:
                act(b, b, AF.Sqrt)
            for (a, b) in chains:
                P.op("dve", lambda e, o=b.ap: e.reciprocal(out=o, in_=o), reads=[b], writes=[b])
            sts[tb] = (hb, rsA, ytok[tb])
        normB(sts[NTB - 1], NTB - 1, next_gidx)

    dma("sp", cbf, dv(c_bf, "c"))
    dma("sp", cf, dv(c_f32, "c"))
    dma("sp", gpre_s, dv(gpre, "c"))
    dma("sp", bg_s, dv(bgate, "c"))
    dma("sp", lng_s, dv(lng, "c"))

    U = [UBASE]

    def ualloc(n):
        off = U[0]
        U[0] += (n + 63) // 64 * 64
        assert U[0] <= POOLB, f"SBUF union overflow {U[0]}"
        return off

    U[0] = Y
    o_ws = ualloc(16 * 128 * 4)
    wsf = sb(o_ws, 16 * 128 * 4, F32, "p (g j) -> p g j", g=16)
    o_wsm = ualloc(16 * 128 * 2)
    wsm = sb(o_wsm, 16 * 128 * 2, BF16, "p (g j) -> p g j", g=16)
    dma("sp", wsf, dv(w_sp.rearrange("g i j -> i g j"), "wsp"))
    tril_b = V(tril.unsqueeze(1).to_broadcast([128, 16, 128]), "sb", cf.lo, cf.hi)
    tt(wsm, wsf, tril_b, ALU.mult)
    for g in range(16):
        b = next_banks(1)[0]
        bv = bank(b, 128, BF16)
        transp(bv, [(bv.ap, wsm.ap[:, g, :])], [wsm])
        cp(V(wsT.ap[:, g, :], "sb", wsT.lo + g * 256, wsT.lo + (g + 1) * 256), bv)
    o_r2 = ualloc(2048 * 4)
    rhs2 = sb(o_r2, 2048 * 4, F32, parts=2)
    o_l2 = ualloc(2048 * 4)
    lhs2 = sb(o_l2, 2048 * 4, F32, parts=2)
    o, = (lhs2.ap,)
    P.op("dve", lambda e: e.memset(lhs2.ap, 1.0), writes=[lhs2])
    dma("sp", V(pool[0:1, o_l2 // 2:(o_l2 + 8192) // 2].bitcast(F32), "sb", lhs2.lo, lhs2.hi), dv(lnb_row, "c"))
    dma("sp", V(pool[1:2, o_r2 // 2:(o_r2 + 8192) // 2].bitcast(F32), "sb", rhs2.lo, rhs2.hi), dv(bsp_row, "c"))
    for q in range(4):
        b = next_banks(1)[0]
        bv = bank(b)
        mm(bv, [(bv.ap[0:1, j * 128:(j + 1) * 128], ones[:, 0:1], wsT.ap[:, q * 4 + j, :], True, True) for j in range(4)],
           [wsT, cbf])
        cp(V(rhs2.ap[0:1, q * 512:(q + 1) * 512], "sb", rhs2.lo, rhs2.hi), V(bv.ap[0:1, :], "ps", bv.lo, bv.hi), eng="act")
    for g in range(16):
        b = next_banks(1)[0]
        bv = bank(b)
        mm(bv, [(bv.ap[:, 0:128], lhs2.ap[:, g * 128:(g + 1) * 128], rhs2.ap[:, g * 128:(g + 1) * 128], True, True)],
           [lhs2, rhs2])
        cp(V(Rs.ap[:, g, :], "sb", Rs.lo + g * 512, Rs.lo + (g + 1) * 512), V(bv.ap[:, 0:128], "ps", bv.lo, bv.hi))

    norm_to_hT(lambda tb: dv(memb[tb * 128:(tb + 1) * 128, :], "memb"), 3, ntb=2)

    def ev_km(j, bv):
        cp(V(kmT.ap[:, j, :], "sb", kmT.lo + j * 512, kmT.lo + (j + 1) * 512), bv, eng="act")
    projF(hT_chunks, w_xk, 0, ev_km, ntok=256)

    def ev_vm(tb, bv):
        cp(V(vm.ap[:, tb, :], "sb", vm.lo + tb * 1024, vm.lo + (tb + 1) * 1024), bv, eng="act")
    projT(hT_chunks, w_xv, 0, 0, ev_vm, ntb=2)

    vgs = [sb(Y + tb * 8192, 8192, F32) for tb in range(NTB)]
    o_uT = Y + NTB * 8192
    uT = sb(o_uT, 16 * TT * 2, BF16, "p (c t) -> p c t", c=16)
    o_vtok = o_uT + 16 * TT * 2
    vtok = sb(o_vtok, NTB * 2048 * 2, BF16, "p (b c) -> p b c", b=NTB)
    assert o_vtok + NTB * 2048 * 2 <= Y + NTB * D * 4
    U[0] = Y
    o_QT = ualloc(24 * TT * 2)
    QT = sb(o_QT, 24 * TT * 2, BF16, "p (c t) -> p c t", c=24)
    kwin = [sb(ualloc((W + TT) * 2), (W + TT) * 2) for (W, _d) in GROUPS]
    vwin = [sb(ualloc((W + TT) * 2), (W + TT) * 2, BF16, "p (b c) -> p b c", c=128) for (W, _d) in GROUPS]
    kraw = [sb(ualloc(TT * 2), TT * 2) for _ in range(4)]
    rtmp = [sb(ualloc(TT * 4), TT * 4, F32) for _ in range(4)]
    pT = [sb(ualloc(TT * 2), TT * 2) for _ in range(4)]
    rden = sb(ualloc(TT * 4), TT * 4, F32)
    vst = [sb(ualloc(512 * 2), 512 * 2) for _ in range(3)]
    assert U[0] <= Y + NTB * D * 4, "ytok-region overflow"
    assert 16 * TT * 2 <= D * 4
    o_ybT = o_xt
    ybT = sb(o_ybT, 16 * TT * 2, BF16, "p (c t) -> p c t", c=16)
    assert 4 * TT * 4 <= D * 2
    sa = sb(o_hn, 4 * TT * 4, F32, "p (c t) -> p c t", c=4)
    U0, U1, U2, U3 = UBASE, UBASE + 8192, UBASE + 16384, UBASE + 20480
    o_yaT = U0
    yaT = sb(U0, 8 * TT * 2, BF16, "p (c t) -> p c t", c=8)
    sbv = sb(U1, 4 * TT * 4, F32, "p (c t) -> p c t", c=4)
    mT = [sb(U2, 4 * TT * 2, BF16, "p (c t) -> p c t", c=4), sb(U3, 4 * TT * 2, BF16, "p (c t) -> p c t", c=4)]
    sgt = sb(U2, TT * 4, F32)
    angf = sb(U1, TT * 4, F32)
    angi = sb(U1 + TT * 4, TT * 4, I32)
    angk = sb(U1 + 2 * TT * 4, TT * 4, F32)
    gbc = sb(U1, D * 4, F32)
    ang_main = (angf, angi, angk)
    qxT = sb(U0, 4 * TT * 2, BF16, "p (c t) -> p c t", c=4)
    oxT = sb(U0 + 4 * TT * 2, 4 * TT * 2, BF16, "p (c t) -> p c t", c=4)
    pX = [sb(U1 + i * TT * 2, TT * 2) for i in range(4)]
    rdenX = sb(U1 + 4 * TT * 2, TT * 4, F32)
    aT = [sb(U0, 8 * TT * 2, BF16, "p (c t) -> p c t", c=8), sb(U1, 8 * TT * 2, BF16, "p (c t) -> p c t", c=8)]
    rl = [sb(U2 + i * TT * 4, TT * 4, F32) for i in range(3)]
    assert U2 + 3 * TT * 4 <= UBASE + USIZE and U1 + D * 4 <= UBASE + USIZE

    rr = {"kraw": 0, "rtmp": 0, "vst": 0, "pT": 0, "rl": 0}

    def rot(lst, key):
        v = lst[rr[key] % len(lst)]
        rr[key] += 1
        return v

    ang_alt = (sb(o_hn, TT * 4, F32), sb(o_hn + TT * 4, TT * 4, I32), sb(o_hn + 2 * TT * 4, TT * 4, F32))

    def rope_tables(tok0, alt=False):
        angf, angi, angk = ang_alt if alt else ang_main
        dma("sp", angi, dv(pos[0:1, tok0:tok0 + TT].partition_broadcast(128)[:, 0, :], "pos"))
        cp(angf, angi)
        ts(angf, angf, invf, None, ALU.mult, extra_r=[cf])
        for (dst, shift) in ((ropeS, 0.0), (ropeC, PI / 2)):
            ts(angk, angf, shift, 1.0 / TWO_PI, ALU.add, ALU.mult)
            cp(angi, angk)
            cp(angk, angi)
            stt(angk, angk, -TWO_PI, angf, ALU.mult, ALU.add)
            ts(angk, angk, shift, None, ALU.add)
            ts(angk, angk, PI, -PI, ALU.min, ALU.max)
            act(dst, angk, AF.Sin)

    def rope_evac(bv, dstv, dst_ap):
        kr = dstv
        act(V(dst_ap, dstv.space, dstv.lo, dstv.hi), bv, AF.Copy)
        b = next_banks(1)[0]
        rb = bank(b)
        mm(rb, [(rb.ap[0:32, 0:TT], ropeP[:, 0:32], dst_ap, True, True)], [kr, cbf])
        t1 = rot(rtmp, "rtmp")
        t2 = rot(rtmp, "rtmp")
        t1v = V(t1.ap[0:32, :], "sb", t1.lo, t1.hi)
        t2v = V(t2.ap[0:32, :], "sb", t2.lo, t2.hi)
        tt(t1v, V(rb.ap[0:32, 0:TT], "ps", rb.lo, rb.hi), V(ropeS.ap[0:32, :], "sb", ropeS.lo, ropeS.hi), ALU.mult)
        d32 = V(dst_ap[0:32, :], dstv.space, dstv.lo, dstv.hi)
        tt(t2v, d32, V(ropeC.ap[0:32, :], "sb", ropeC.lo, ropeC.hi), ALU.mult)
        tt(d32, t1v, t2v, ALU.add)

    def mixer_kv(tok0, glist):
        for g in glist:
            for half in range(2):
                c0 = 3072 + g * 1024 + half * 512

                def ev_k(j, bv, g=g, half=half):
                    kr = rot(kraw, "kraw")
                    rope_evac(bv, kr, kr.ap)
                    dma("sp", dv(kth[g, half * 4 + j, :, tok0:tok0 + TT], f"kth{g}"), kr)
                projF(hT_chunks, w_in, c0, ev_k)
            for half in range(2):
                c0 = 6144 + g * 1024 + half * 512

                def ev_v(tb, bv, g=g, half=half):
                    st = rot(vst, "vst")
                    cp(st, bv, eng="act")
                    dma("sp", dv(vh[g, tok0 + tb * 128:tok0 + (tb + 1) * 128, half * 512:(half + 1) * 512], f"vh{g}"), st)
                projT(hT_chunks, w_in, 0, c0, ev_v)

    def attention(tok0):
        S_BANKS = (0, 1, 2, 3, 4, 5)
        for h in range(8):
            Ob, Db = bank(6), bank(7)
            blocks = []
            for g, (W, dil) in enumerate(GROUPS):
                nkb = W // 128
                kw, vw = kwin[g], vwin[g]
                ntok = W + TT
                kwv = V(kw.ap[:, 0:ntok], "sb", kw.lo, kw.hi)
                dma("sp", kwv, dv(kth[g, h, :, tok0 - W:tok0 + TT], f"kth{g}"))
                vwv = V(vw.ap[:, 0:ntok // 128, :], "sb", vw.lo, vw.hi)
                dma("sp", vwv, dv(vh[g, tok0 - W:tok0 + TT, h * 128:(h + 1) * 128].rearrange("(b p) c -> p b c", p=128), f"vh{g}"))
                for m in range(-nkb, NTB):
                    qlo, qhi = max(m, 0), min(m + nkb, NTB - 1)
                    blocks.append((g, dil, nkb, m, qlo, qhi, kwv, vwv))
            first = [True]
            pend = []

            def emit_pv(item):
                (g, dil, nkb, m, qlo, qhi, kwv, vwv, pt, n) = item
                st = first[0]
                first[0] = False
                mm(Ob, [(Ob.ap[:, qlo * 128:qlo * 128 + n], vwv.ap[:, m + nkb, :], pt.ap[:, 0:n], st, False)], [vwv, pt])
                mm(Db, [(Db.ap[:, qlo * 128:qlo * 128 + n], ones, pt.ap[:, 0:n], st, False)], [pt, cbf])

            for (g, dil, nkb, m, qlo, qhi, kwv, vwv) in blocks:
                n = (qhi - qlo + 1) * 128
                b = next_banks(1, S_BANKS)[0]
                Sb = bank(b)
                mms = [(Sb.ap[:, 0:n], kwv.ap[:, (m + nkb) * 128:(m + nkb + 1) * 128], QT.ap[:, g * 8 + h, qlo * 128:qlo * 128 + n], True, False)]
                for qb in range(qlo, qhi + 1):
                    dl = qb - m
                    typ = 0 if dl == 0 else (1 if dl == nkb else 2)
                    if not (typ == 2 and dil == 1):
                        mms.append((Sb.ap[:, (qb - qlo) * 128:(qb - qlo + 1) * 128], ident, maskap(g, typ), False, False))
                if tok0 + 128 * m < HALO:
                    for qb in range(qlo, qhi + 1):
                        mms.append((Sb.ap[:, (qb - qlo) * 128:(qb - qlo + 1) * 128], ident, halob, False, False))
                mm(Sb, mms, [kwv, QT, cbf])
                pt = rot(pT, "pT")
                act(V(pt.ap[:, 0:n], "sb", pt.lo, pt.hi), V(Sb.ap[:, 0:n], "ps", Sb.lo, Sb.hi), AF.Exp, scale=SCALE)
                pend.append((g, dil, nkb, m, qlo, qhi, kwv, vwv, pt, n))
                if len(pend) > 1:
                    emit_pv(pend.pop(0))
            while pend:
                emit_pv(pend.pop(0))
            P.op("dve", lambda e, o=rden.ap, i=Db.ap[:, 0:TT]: e.reciprocal(out=o, in_=i), reads=[Db], writes=[rden])
            tt(V(yaT.ap[:, h, :], "sb", yaT.lo + h * TT * 2, yaT.lo + (h + 1) * TT * 2), V(Ob.ap[:, 0:TT], "ps", Ob.lo, Ob.hi), rden, ALU.mult)

    def mixer_full(tok0):
        row0 = tok0 - HALO
        for q in range(4):
            def ev_v2(tb, bv, q=q):
                act(V(vgs[tb].ap[:, q * 512:(q + 1) * 512], "sb", vgs[tb].lo + q * 2048, vgs[tb].lo + (q + 1) * 2048), bv,
                    AF.Gelu_apprx_tanh)
            projT(hT_chunks, w_in, 0, 11264 + q * 512, ev_v2)
        for tb in range(NTB):
            s4 = V(small.ap[:, 16 + tb * 4:20 + tb * 4], "sb", small.lo + (16 + tb * 4) * 4, small.lo + (20 + tb * 4) * 4)
            mean = sc(32 + tb)
            act(V(hn.ap[:, 0:2048], 'sb', hn.lo, hn.hi), vgs[tb], AF.Identity, accum=mean)
            ts(mean, mean, -1.0 / 2048, None, ALU.mult)
            ssq = sc(36 + tb)
            act(V(hn.ap[:, 0:2048], 'sb', hn.lo, hn.hi), vgs[tb], AF.Square, accum=ssq)
            var = sc(40 + tb)
            m2 = sc(44 + tb)
            ts(m2, mean, mean.ap, -1.0, ALU.mult, ALU.mult, extra_r=[mean])
            ts(var, ssq, 1.0 / 2048, EPS, ALU.mult, ALU.add)
            act(var, var, AF.Sqrt, bias=m2.ap, extra_r=[m2])
            P.op("dve", lambda e, o=var.ap: e.reciprocal(out=o, in_=o), reads=[var], writes=[var])
            ts(V(vtok.ap[:, tb, :], "sb", vtok.lo + tb * 4096, vtok.lo + (tb + 1) * 4096), vgs[tb], mean.ap, var.ap,
               ALU.add, ALU.mult, extra_r=[mean, var])
        for q in range(4):
            def ev_u(j, bv, q=q):
                c = q * 4 + j
                act(V(uT.ap[:, c, :], "sb", uT.lo + c * TT * 2, uT.lo + (c + 1) * TT * 2), bv, AF.Gelu_apprx_tanh)
            projF(hT_chunks, w_in, 9216 + q * 512, ev_u)
        for cg in range(16):
            b = next_banks(1)[0]
            bv = bank(b)
            mm(bv, [(bv.ap[:, tb * 128:(tb + 1) * 128], vtok.ap[:, tb, cg * 128:(cg + 1) * 128], wsT.ap[:, cg, :], True, True)
                    for tb in range(NTB)], [vtok, wsT])
            b3 = bv.ap[:, 0:TT].rearrange("p (b i) -> p b i", b=NTB)
            r3 = Rs.ap[:, cg, :].unsqueeze(1).to_broadcast([128, NTB, 128])
            s3 = sgt.ap.rearrange("p (b i) -> p b i", b=NTB)
            stt(V(s3, "sb", sgt.lo, sgt.hi), V(b3, "ps", bv.lo, bv.hi), lng_s.ap[:, cg:cg + 1], V(r3, "sb", Rs.lo, Rs.hi),
                ALU.mult, ALU.add, extra_r=[lng_s])
            tt(V(ybT.ap[:, cg, :], "sb", ybT.lo + cg * TT * 2, ybT.lo + (cg + 1) * TT * 2), sgt,
               V(uT.ap[:, cg, :], "sb", uT.lo + cg * TT * 2, uT.lo + (cg + 1) * TT * 2), ALU.mult)
        for g in range(3):
            for half in range(2):
                def ev_q(j, bv, g=g, half=half):
                    c = g * 8 + half * 4 + j
                    rope_evac(bv, V(QT.ap, "sb", QT.lo + c * TT * 2, QT.lo + (c + 1) * TT * 2), QT.ap[:, c, :])
                projF(hT_chunks, w_in, g * 1024 + half * 512, ev_q)
        attention(tok0)
        if dbg.get("dump") and tok0 == HALO:
            dma("sp", dv(dbgf, "dbgf"), small)
            dma("sp", dv(dbgo[:, 0:2048], "dbg"), V(pool[:, o_yaT // 2:o_yaT // 2 + 2048], "sb", yaT.lo, yaT.hi))
            dma("sp", dv(dbgo[:, 2048:6144], "dbg"), V(pool[:, o_ybT // 2:o_ybT // 2 + 4096], "sb", ybT.lo, ybT.hi))
            dma("sp", dv(dbgo[:, 6144:10240], "dbg"), V(pool[:, o_uT // 2:o_uT // 2 + 4096], "sb", uT.lo, uT.hi))
            dma("sp", dv(dbgo[:, 10240:16384], "dbg"), V(pool[:, o_QT // 2:o_QT // 2 + 6144], "sb", QT.lo, QT.hi))
        ya_chunks = [(yaT, yaT.ap[:, c, :]) for c in range(8)]
        yb_chunks = [(ybT, ybT.ap[:, c, :]) for c in range(16)]
        for cg in range(8):
            def ev_ga(j, bv, cg=cg):
                act(V(sa.ap[:, j, :], "sb", sa.lo + j * TT * 4, sa.lo + (j + 1) * TT * 4), bv, AF.Sigmoid,
                    bias=bg_s.ap[:, cg * 4 + j:cg * 4 + j + 1], extra_r=[bg_s])
            projF(hT_chunks, w_gate, cg * 512, ev_ga)

            def ev_gb(j, bv, cg=cg):
                act(V(sbv.ap[:, j, :], "sb", sbv.lo + j * TT * 4, sbv.lo + (j + 1) * TT * 4), bv, AF.Sigmoid,
                    bias=bg_s.ap[:, 32 + cg * 4 + j:32 + cg * 4 + j + 1], extra_r=[bg_s])
            projF(hT_chunks, w_gate, D + cg * 512, ev_gb)

            def ev_a(j, bv):
                v_ = V(sa.ap[:, j, :], "sb", sa.lo + j * TT * 4, sa.lo + (j + 1) * TT * 4)
                tt(v_, bv, v_, ALU.mult)
            projF(ya_chunks, w_ba, cg * 512, ev_a)
            mt = mT[cg % 2]

            def ev_b(j, bv, mt=mt):
                v_ = V(sbv.ap[:, j, :], "sb", sbv.lo + j * TT * 4, sbv.lo + (j + 1) * TT * 4)
                tt(v_, bv, v_, ALU.mult)
                tt(V(mt.ap[:, j, :], "sb", mt.lo + j * TT * 2, mt.lo + (j + 1) * TT * 2),
                   V(sa.ap[:, j, :], "sb", sa.lo + j * TT * 4, sa.lo + (j + 1) * TT * 4), v_, ALU.add)
            projF(yb_chunks, w_bb, cg * 512, ev_b)
            m_chunks = [(mt, mt.ap[:, c, :]) for c in range(4)]
            for ocg in range(8):
                def ev_o(tb, bv, ocg=ocg, cg=cg):
                    yv = V(ytok[tb].ap[:, ocg * 512:(ocg + 1) * 512], "sb", ytok[tb].lo + ocg * 2048, ytok[tb].lo + (ocg + 1) * 2048)
                    if cg == 0:
                        cp(yv, bv, eng="act")
                    else:
                        tt(yv, bv, yv, ALU.add)
                projT(m_chunks, w_out, cg * 512, ocg * 512, ev_o)
        post_residual(lambda tb: dv(xs[tok0 + tb * 128:tok0 + (tb + 1) * 128, :], "xs"),
                      lambda tb: dv(out[row0 + tb * 128:row0 + (tb + 1) * 128, :], f"out{row0 + tb * 128}"), 0, next_gidx=1)

    def xa_phase(row0):
        rows = lambda tb: dv(out[row0 + tb * 128:row0 + (tb + 1) * 128, :], f"out{row0 + tb * 128}")

        def ev_q(j, bv):
            cp(V(qxT.ap[:, j, :], "sb", qxT.lo + j * TT * 2, qxT.lo + (j + 1) * TT * 2), bv, eng="act")
        projF(hT_chunks, w_xq, 0, ev_q)
        for h in range(4):
            Ob, Db = bank(6), bank(7)
            pts = []
            for mb in range(2):
                b = next_banks(1, (0, 1, 2, 3, 4, 5))[0]
                Sb = bank(b)
                mm(Sb, [(Sb.ap[:, 0:TT], kmT.ap[:, h, mb * 128:(mb + 1) * 128], qxT.ap[:, h, :], True, True)], [kmT, qxT])
                pt = rot(pX, "pT")
                act(pt, V(Sb.ap[:, 0:TT], "ps", Sb.lo, Sb.hi), AF.Exp, scale=SCALE)
                pts.append(pt)
            for mb in range(2):
                mm(Ob, [(Ob.ap[:, 0:TT], vm.ap[:, mb, h * 128:(h + 1) * 128], pts[mb].ap, mb == 0, mb == 1)], [vm, pts[mb]])
                mm(Db, [(Db.ap[:, 0:TT], ones, pts[mb].ap, mb == 0, mb == 1)], [pts[mb], cbf])
            P.op("dve", lambda e, o=rdenX.ap, i=Db.ap[:, 0:TT]: e.reciprocal(out=o, in_=i), reads=[Db], writes=[rdenX])
            tt(V(oxT.ap[:, h, :], "sb", oxT.lo + h * TT * 2, oxT.lo + (h + 1) * TT * 2), V(Ob.ap[:, 0:TT], "ps", Ob.lo, Ob.hi), rdenX, ALU.mult)
        ox_chunks = [(oxT, oxT.ap[:, c, :]) for c in range(4)]
        for ocg in range(8):
            def ev_o(tb, bv, ocg=ocg):
                cp(V(ytok[tb].ap[:, ocg * 512:(ocg + 1) * 512], "sb", ytok[tb].lo + ocg * 2048, ytok[tb].lo + (ocg + 1) * 2048), bv, eng="act")
            projT(ox_chunks, w_xo, 0, ocg * 512, ev_o)
        post_residual(rows, rows, 1, next_gidx=2)

    def mlp_phase(row0, early_next=None):
        rows = lambda tb: dv(out[row0 + tb * 128:row0 + (tb + 1) * 128, :], f"out{row0 + tb * 128}")
        for sec in range(DFF // SEC):
            at = aT[sec % 2]
            if sec == DFF // SEC - 1 and early_next is not None:
                early_next["pre"]()
            for half in range(SEC // 512):
                def ev_up(j, bv, half=half, at=at):
                    r = rot(rl, "rl")
                    act(r, bv, AF.Relu)
                    c = half * 4 + j
                    tt(V(at.ap[:, c, :], "sb", at.lo + c * TT * 2, at.lo + (c + 1) * TT * 2), r, r, ALU.mult)
                projF(hT_chunks, w_up, sec * SEC + half * 512, ev_up)
            a_chunks = [(at, at.ap[:, c, :]) for c in range(SEC // 128)]
            if sec == DFF // SEC - 1 and early_next is not None:
                early_next["step"](0)
            for ocg in range(8):
                def ev_dn(tb, bv, ocg=ocg, sec=sec):
                    yv = V(ytok[tb].ap[:, ocg * 512:(ocg + 1) * 512], "sb", ytok[tb].lo + ocg * 2048, ytok[tb].lo + (ocg + 1) * 2048)
                    if sec == 0:
                        cp(yv, bv, eng="act")
                    else:
                        tt(yv, bv, yv, ALU.add)
                projT(a_chunks, w_down, sec * SEC, ocg * 512, ev_dn)
                if sec == DFF // SEC - 1 and early_next is not None and ocg in (1, 3, 5) and (ocg + 1) // 2 < NTB:
                    early_next["step"]((ocg + 1) // 2)
        post_residual(rows, rows, 2)

    state = {}
    for t in range(NHT + NT):
        tok0 = t * TT
        is_halo = t < NHT
        if is_halo:
            glist = [g for g, (W, dil) in enumerate(GROUPS) if HALO - (tok0 + TT) < W]
        else:
            glist = [0, 1, 2]
        xrows = lambda tb, tok0=tok0: dv(xs[tok0 + tb * 128:tok0 + (tb + 1) * 128, :], "xs")
        if not state.get("prenormed"):
            rope_tables(tok0)
            norm_to_hT(xrows, 0, preloaded=state.get("preloaded", False))
        state["prenormed"] = False
        state["preloaded"] = False
        if is_halo and t + 1 < NHT + NT:
            ntok0 = (t + 1) * TT
            preload_x(lambda tb, ntok0=ntok0: dv(xs[ntok0 + tb * 128:ntok0 + (tb + 1) * 128, :], "xs"))
            state["preloaded"] = True
        mixer_kv(tok0, glist)
        if not is_halo:
            mixer_full(tok0)
            if dbg.get("stop") == "mixer":
                break
            xa_phase(tok0 - HALO)
            if dbg.get("stop") == "xa":
                break
            early = None
            if t + 1 < NHT + NT and (not dbg.get("ntiles") or (t - NHT + 1) < dbg["ntiles"]):
                ntok0 = (t + 1) * TT

                def mk_early(ntok0):
                    xr = lambda tb: dv(xs[ntok0 + tb * 128:ntok0 + (tb + 1) * 128, :], "xs")
                    est = {}

                    def pre():
                        rope_tables(ntok0, alt=True)
                        dma("sp", xt, xr(0))
                        est[0] = normA(xt, 48, 50)
                        normB_id(est[0])
                        if NTB > 1:
                            dma("sp", xt, xr(1))

                    def step(tb):
                        normB_tr(est[tb], tb, 0)
                        if tb + 1 < NTB:
                            est[tb + 1] = normA(xt, 48 + ((tb + 1) % 2), 50 + ((tb + 1) % 2))
                            normB_id(est[tb + 1])
                            if tb + 2 < NTB:
                                dma("sp", xt, xr(tb + 2))
                        else:
                            state["prenormed"] = True
                    return {"pre": pre, "step": step}
                early = mk_early(ntok0)
            mlp_phase(tok0 - HALO, early)
            if dbg.get("ntiles") and t - NHT + 1 >= dbg["ntiles"]:
                break

    final_waits = [(d.sem, d.count) for d in P.dsems + wsems if d.count > 0]

    with nc.Block() as block:
        def replay(e, eng):
            for waits, fn, sem, inc in eng.prog:
                for (s, v) in waits:
                    e.wait_ge(s, v)
                ins = fn(e)
                ins.then_inc(sem, inc)

        @block.tensor
        def _(e):
            replay(e, P.engs["pe"])

        @block.scalar
        def _(e):
            replay(e, P.engs["act"])

        @block.vector
        def _(e):
            replay(e, P.engs["dve"])

        @block.gpsimd
        def _(e):
            replay(e, P.engs["pool"])

        @block.sync
        def _(e):
            replay(e, P.engs["sp"])
            for (s, v) in final_waits:
                e.wait_ge(s, v)
            for nm in ("pe", "act", "dve", "pool"):
                en = P.engs[nm]
                if en.cnt:
                    e.wait_ge(en.sem, en.cnt)
    es.close()
    return nc


def _consts(core):
    bf = ml_dtypes.bfloat16
    c = np.zeros((128, 13, 128), np.float32)
    c[:, 0] = np.eye(128)
    c[:, 1] = 1.0
    for i in range(16):
        c[i + 16, 2, i] = -1.0
        c[i, 2, i + 16] = 1.0
    k = np.arange(128)[:, None]
    q = np.arange(128)[None, :]
    for g, (W, dil) in enumerate(GROUPS):
        res = ((q - k) % dil) == 0
        c[:, 3 + g * 3 + 0] = np.where(res & (q - k >= 0), 0.0, NEGM)
        c[:, 3 + g * 3 + 1] = np.where(res & (q - k <= 0), 0.0, NEGM)
        c[:, 3 + g * 3 + 2] = np.where(res, 0.0, NEGM)
    c[:, 12] = NEGM if core % 2 == 0 else 0.0
    cf = np.zeros((128, 129), np.float32)
    cf[:, :128] = np.tril(np.ones((128, 128), np.float32))
    inv = (np.float32(500000.0) ** (-np.arange(0, 32, 2, dtype=np.float32) / np.float32(32))).astype(np.float32)
    cf[0:16, 128] = inv
    cf[16:32, 128] = inv
    return c.reshape(128, 13 * 128).astype(bf), cf


_NC_CACHE = {}


def kernel(x, mem, positions, mix_pre_g, w_in, sgu_ln_g, sgu_ln_b, w_spatial, b_spatial,
           w_branch_a, w_branch_b, w_gate, b_gate, w_out, mix_post_g, xa_pre_g, mem_norm_g,
           w_xq, w_xk, w_xv, w_xo, xa_post_g, mlp_pre_g, w_up, w_down, mlp_post_g):
    f = lambda a: np.ascontiguousarray(np.asarray(a))
    x = np.asarray(x)
    positions = np.asarray(positions)
    col = lambda g: np.asarray(g)[0].reshape(-1, 128).T
    gpre = f(np.concatenate([col(mix_pre_g), col(xa_pre_g), col(mlp_pre_g), col(mem_norm_g)], axis=1).astype(np.float32))
    gpost = f(np.stack([np.asarray(mix_post_g)[0], np.asarray(xa_post_g)[0], np.asarray(mlp_post_g)[0]]).astype(np.float32))
    shared = {
        "w_in": f(np.asarray(w_in)[0]), "w_gate": f(np.asarray(w_gate)[0]), "w_branch_a": f(np.asarray(w_branch_a)[0]),
        "w_branch_b": f(np.asarray(w_branch_b)[0]), "w_out": f(np.asarray(w_out)[0]), "w_xq": f(np.asarray(w_xq)[0]),
        "w_xk": f(np.asarray(w_xk)[0]), "w_xv": f(np.asarray(w_xv)[0]), "w_xo": f(np.asarray(w_xo)[0]),
        "w_up": f(np.asarray(w_up)[0]), "w_down": f(np.asarray(w_down)[0]), "w_spatial": f(np.asarray(w_spatial)[0]),
        "gpre": gpre, "gpost": gpost, "bgate": f(col(b_gate)), "lng": f(col(sgu_ln_g)),
        "lnb_row": f(np.asarray(sgu_ln_b)[0].reshape(1, 2048)), "bsp_row": f(np.asarray(b_spatial)[0].reshape(1, 2048)),
    }
    in_maps = []
    for c in range(8):
        b, half = c // 2, c % 2
        own = x[b, half * OWN:(half + 1) * OWN]
        if half == 1:
            halo = x[b, 0:HALO]
            ph = positions[b, 0:HALO]
        else:
            halo = np.zeros((HALO, D), np.float32)
            ph = np.zeros((HALO,), np.int32)
        cb, cf = _consts(c)
        m = dict(shared)
        m["xs"] = f(np.concatenate([halo, own], axis=0))
        m["pos"] = f(np.concatenate([ph, positions[b, half * OWN:(half + 1) * OWN]]).reshape(1, -1).astype(np.int32))
        m["memb"] = f(np.asarray(mem)[b])
        m["c_bf"] = cb
        m["c_f32"] = cf
        in_maps.append(m)
    dbg = _NC_CACHE.get("dbg")
    if "nc" not in _NC_CACHE:
        _NC_CACHE["nc"] = build_program(dbg)
    nc = _NC_CACHE["nc"]
    ncores = (dbg or {}).get("ncores", 8)
    res = run_bass_kernel_spmd(nc, in_maps[:ncores], core_ids=list(range(ncores)))
    if ncores < 8:
        _NC_CACHE["last"] = res.results[0]
        return res.results[0]["out"]
    outp = np.empty((4, 4096, D), np.float32)
    for c in range(8):
        b, half = c // 2, c % 2
        outp[b, half * OWN:(half + 1) * OWN] = res.results[c]["out"]
    return outp
```

```python
import numpy as np
import ml_dtypes
from contextlib import ExitStack
import concourse.bass as bass
import concourse.mybir as mybir
from concourse.bass_utils import run_bass_kernel_spmd

F32 = mybir.dt.float32
BF16 = mybir.dt.bfloat16
I32 = mybir.dt.int32
AF = mybir.ActivationFunctionType
ALU = mybir.AluOpType

D = 4096
TT = 512
NTB = TT // 128
OWN = 2048
HALO = 2048
NT = OWN // TT
NHT = HALO // TT
GROUPS = ((128, 1), (512, 4), (2048, 16))
INW = 13312
DFF = 16384
SEC = 1024
EPS = 1e-6
NEGM = -30000.0
SCALE = 128 ** -0.5
RING = 4
POOL_ENG = "pool"
PI = float(np.pi)
TWO_PI = float(2 * np.pi)


class Eng:
    def __init__(self, name, sem, selfsync):
        self.name, self.sem, self.cnt, self.prog, self.seen, self.selfsync = name, sem, 0, [], {}, selfsync


class DSem:
    def __init__(self, sem):
        self.sem, self.count = sem, 0


class V:
    def __init__(self, ap, space, lo, hi):
        self.ap, self.space, self.lo, self.hi = ap, space, lo, hi


class Tracker:
    def __init__(self):
        self.recs = {}

    def access(self, v, is_w, ev, engname):
        lst = self.recs.setdefault(v.space, [])
        deps = []
        keep = []
        for r in lst:
            lo, hi, rev, rw, rname = r
            if lo < v.hi and v.lo < hi:
                if is_w or rw:
                    deps.append(rev)
                if is_w and v.lo <= lo and hi <= v.hi:
                    continue
                if (not is_w) and (not rw) and rev[0] is ev[0] and lo == v.lo and hi == v.hi:
                    continue
            keep.append(r)
        keep.append((v.lo, v.hi, ev, is_w, engname))
        self.recs[v.space] = keep
        return deps


class Prog:
    def __init__(self):
        self.tr = Tracker()
        self.engs = {}
        self.dsems = []
        self.dsi = 0

    def next_dsem(self):
        d = self.dsems[self.dsi % len(self.dsems)]
        self.dsi += 1
        return d

    def op(self, engname, fn, reads=(), writes=(), dsem=None):
        eng = self.engs[engname]
        if dsem is not None:
            ev = (dsem.sem, dsem.count + 16)
        else:
            ev = (eng.sem, eng.cnt + 1)
        deps = {}
        for v in reads:
            for (s, val) in self.tr.access(v, False, ev, engname):
                deps[s] = max(deps.get(s, 0), val)
        for v in writes:
            for (s, val) in self.tr.access(v, True, ev, engname):
                deps[s] = max(deps.get(s, 0), val)
        if dsem is not None and dsem.count > 0:
            deps[dsem.sem] = max(deps.get(dsem.sem, 0), dsem.count)
        waits = []
        for s, val in deps.items():
            if s is ev[0] and val >= ev[1]:
                continue
            if s is eng.sem and not eng.selfsync:
                continue
            if eng.seen.get(id(s), 0) >= val:
                continue
            eng.seen[id(s)] = val
            waits.append((s, val))
        if dsem is not None:
            dsem.count += 16
            eng.prog.append((waits, fn, dsem.sem, 16))
        else:
            eng.cnt += 1
            eng.prog.append((waits, fn, eng.sem, 1))
        return ev


def build_program(dbg=None):
    dbg = dbg or {}
    nc = bass.Bass("TRN2", target_bir_lowering=False)
    P = Prog()
    es = ExitStack()

    def din(name, shape, dt=F32):
        return nc.dram_tensor(name, list(shape), dt, kind="ExternalInput").ap()

    xs = din("xs", [HALO + OWN, D])
    pos = din("pos", [1, HALO + OWN], I32)
    memb = din("memb", [256, D])
    w_in = din("w_in", [D, INW])
    w_gate = din("w_gate", [D, 2 * D])
    w_ba = din("w_branch_a", [1024, D])
    w_bb = din("w_branch_b", [2048, D])
    w_out = din("w_out", [D, D])
    w_xq = din("w_xq", [D, 512])
    w_xk = din("w_xk", [D, 512])
    w_xv = din("w_xv", [D, 512])
    w_xo = din("w_xo", [512, D])
    w_up = din("w_up", [D, DFF])
    w_down = din("w_down", [DFF, D])
    w_sp = din("w_spatial", [16, 128, 128])
    gpre = din("gpre", [128, 4 * 32])
    gpost = din("gpost", [3, D])
    bgate = din("bgate", [128, 64])
    lng = din("lng", [128, 16])
    lnb_row = din("lnb_row", [1, 2048])
    bsp_row = din("bsp_row", [1, 2048])
    c_bf = din("c_bf", [128, 13 * 128], BF16)
    c_f32 = din("c_f32", [128, 129])
    out = nc.dram_tensor("out", [OWN, D], F32, kind="ExternalOutput").ap()
    if dbg.get("dump"):
        dbgo = nc.dram_tensor("dbgo", [128, 16384], BF16, kind="ExternalOutput").ap()
        dbgf = nc.dram_tensor("dbgf", [128, 64], F32, kind="ExternalOutput").ap()
    kth = nc.dram_tensor("kth", [3, 8, 128, HALO + OWN], BF16, kind="Internal").ap()
    vh = nc.dram_tensor("vh", [3, HALO + OWN, 1024], BF16, kind="Internal").ap()

    def dv(ap, name):
        return V(ap, "d:" + name, 0, 1)

    POOLB = 206 * 1024
    pool = es.enter_context(nc.sbuf_tensor("pool", [128, POOLB // 2], BF16))
    ps = es.enter_context(nc.psum_tensor("ps", [128, 8 * 512], F32))
    cur = [0]

    def alloc(nbytes):
        off = cur[0]
        cur[0] += (nbytes + 63) // 64 * 64
        assert cur[0] <= POOLB, f"SBUF overflow {cur[0]}"
        return off

    def sb(off, nbytes, dt=BF16, pat=None, parts=128, **kw):
        ap = pool[0:parts, off // 2:(off + nbytes) // 2]
        if dt != BF16:
            ap = ap.bitcast(dt)
        if pat:
            ap = ap.rearrange(pat, **kw)
        return V(ap, "sb", off, off + nbytes)

    def sub(v, ap, lo_b, hi_b):
        return V(ap, v.space, v.lo + lo_b, v.lo + hi_b)

    def bank(b, n=512, dt=F32):
        ap = ps[:, b * 512:b * 512 + (n if dt == F32 else n // 2)]
        if dt != F32:
            ap = ap.bitcast(dt)
        return V(ap, "ps", b * 2048, (b + 1) * 2048)

    o_cbf = alloc(13 * 256)
    cbf = sb(o_cbf, 13 * 256)
    ident = cbf.ap[:, 0:128]
    ones = cbf.ap[:, 128:256]
    ropeP = cbf.ap[:, 256:384]

    def maskap(g, typ):
        i = 3 + g * 3 + typ
        return cbf.ap[:, i * 128:(i + 1) * 128]
    halob = cbf.ap[:, 12 * 128:13 * 128]
    o_cf = alloc(129 * 4)
    cf = sb(o_cf, 129 * 4, F32)
    tril = cf.ap[:, 0:128]
    invf = cf.ap[:, 128:129]
    o_gpre = alloc(128 * 4)
    gpre_s = sb(o_gpre, 128 * 4, F32)
    o_bg = alloc(64 * 4)
    bg_s = sb(o_bg, 64 * 4, F32)
    o_lng = alloc(16 * 4)
    lng_s = sb(o_lng, 16 * 4, F32)
    o_wsT = alloc(16 * 128 * 2)
    wsT = sb(o_wsT, 16 * 128 * 2, BF16, "p (g i) -> p g i", g=16)
    o_R = alloc(16 * 128 * 4)
    Rs = sb(o_R, 16 * 128 * 4, F32, "p (g i) -> p g i", g=16)
    o_kmT = alloc(4 * 256 * 2)
    kmT = sb(o_kmT, 4 * 256 * 2, BF16, "p (h m) -> p h m", h=4)
    o_vm = alloc(2 * 512 * 2)
    vm = sb(o_vm, 2 * 512 * 2, BF16, "p (b c) -> p b c", b=2)
    o_cs = alloc(2 * TT * 4)
    ropeC = sb(o_cs, TT * 4, F32)
    ropeS = sb(o_cs + TT * 4, TT * 4, F32)
    o_small = alloc(64 * 4)
    small = sb(o_small, 64 * 4, F32)
    ring = [sb(alloc(8192), 8192) for _ in range(RING)]
    o_hT = alloc(32 * TT * 2)
    hT = sb(o_hT, 32 * TT * 2, BF16, "p (c t) -> p c t", c=32)
    o_ytok = alloc(NTB * D * 4)
    ytok = [sb(o_ytok + tb * D * 4, D * 4, F32) for tb in range(NTB)]
    o_xt = alloc(D * 4)
    xt = sb(o_xt, D * 4, F32)
    o_hn = alloc(D * 2)
    hn = sb(o_hn, D * 2)
    UBASE = cur[0]
    USIZE = 24 * 1024
    assert UBASE + USIZE <= POOLB, f"SBUF overflow {UBASE + USIZE}"
    Y = o_ytok
    hnb = [hn, sb(UBASE, D * 2)]
    xth = [sb(o_xt, 8192, F32), sb(o_xt + 8192, 8192, F32)]
    xbufs = [xt, ytok[0], ytok[1]]
    nrm_cnt = [0]

    for name, ss in (("pe", False), ("act", True), ("dve", True), ("pool", True), ("sp", True)):
        P.engs[name] = Eng(name, es.enter_context(nc.semaphore("s_" + name)), ss)
    P.dsems = [DSem(es.enter_context(nc.semaphore(f"dq{i}"))) for i in range(24)]
    wsems = [DSem(es.enter_context(nc.semaphore(f"wq{i}"))) for i in range(RING)]

    def dma(q, outv, inv, dsem=None):
        d = dsem or P.next_dsem()
        o_ap, i_ap = outv.ap, inv.ap
        P.op(q, lambda e: e.dma_start(out=o_ap, in_=i_ap), reads=[inv], writes=[outv], dsem=d)

    def act(outv, inv, func, bias=None, scale=None, accum=None, extra_r=()):
        kw = {}
        if bias is not None:
            kw["bias"] = bias
        if scale is not None:
            kw["scale"] = scale
        if accum is not None:
            kw["accum_out"] = accum.ap
        o_ap, i_ap = outv.ap, inv.ap
        P.op("act", lambda e: e.activation(out=o_ap, in_=i_ap, func=func, **kw),
             reads=[inv] + list(extra_r), writes=[outv] + ([accum] if accum is not None else []))

    def tt(outv, av, bv, op, eng="dve"):
        o, a, b = outv.ap, av.ap, bv.ap
        P.op(eng, lambda e: e.tensor_tensor(out=o, in0=a, in1=b, op=op), reads=[av, bv], writes=[outv])

    def ts(outv, av, s1, s2, op0, op1=None, extra_r=(), eng="dve"):
        o, a = outv.ap, av.ap
        if op1 is None:
            P.op(eng, lambda e: e.tensor_scalar(out=o, in0=a, scalar1=s1, scalar2=None, op0=op0),
                 reads=[av] + list(extra_r), writes=[outv])
        else:
            P.op(eng, lambda e: e.tensor_scalar(out=o, in0=a, scalar1=s1, scalar2=s2, op0=op0, op1=op1),
                 reads=[av] + list(extra_r), writes=[outv])

    def stt(outv, av, scal, bv, op0, op1, extra_r=()):
        o, a, b = outv.ap, av.ap, bv.ap
        P.op("dve", lambda e: e.scalar_tensor_tensor(out=o, in0=a, scalar=scal, in1=b, op0=op0, op1=op1),
             reads=[av, bv] + list(extra_r), writes=[outv])

    def cp(outv, inv, eng="dve"):
        if eng == "act":
            return act(outv, inv, AF.Copy)
        o, i = outv.ap, inv.ap
        P.op(eng, lambda e: e.tensor_copy(out=o, in_=i), reads=[inv], writes=[outv])

    def mm(outv, mms, reads):
        def fn(e):
            ins = None
            for (o, l, r, st, sp) in mms:
                ins = e.matmul(o, lhsT=l, rhs=r, start=st, stop=sp, skip_group_check=True)
            return ins
        P.op("pe", fn, reads=reads, writes=[outv])

    def transp(outv, pairs, reads):
        def fn(e):
            ins = None
            for (o, i) in pairs:
                ins = e.transpose(o, i, ident)
            return ins
        P.op("pe", fn, reads=reads + [cbf], writes=[outv])

    wcount = [0]

    def wload(w_ap, k0, nkc, c0, ncols):
        assert nkc * ncols <= 4096
        i = wcount[0] % RING
        wcount[0] += 1
        slot = ring[i]
        ap = slot.ap[:, 0:nkc * ncols].rearrange("p (k n) -> p k n", k=nkc)
        src = w_ap[k0:k0 + nkc * 128, c0:c0 + ncols].rearrange("(k p) n -> p k n", p=128)
        sv = V(ap, "sb", slot.lo, slot.hi)
        dma("pool", sv, V(src, "d:w", 0, 0), dsem=wsems[i])
        return sv

    bank_rr = [0]

    def next_banks(n, pool_list=(0, 1, 2, 3, 4, 5, 6, 7)):
        res = []
        for _ in range(n):
            res.append(pool_list[bank_rr[0] % len(pool_list)])
            bank_rr[0] += 1
        return res

    def projF(lhs_chunks, w_ap, c0, evac, k0=0, ntok=TT):
        nk = len(lhs_chunks)
        bks = next_banks(4)
        bvs = [bank(b) for b in bks]
        for s0 in range(0, nk, 8):
            n = min(8, nk - s0)
            sv = wload(w_ap, k0 + s0 * 128, n, c0, 512)
            for j in range(4):
                mms = []
                rd = [sv]
                for kk in range(n):
                    kc = s0 + kk
                    lv, lap = lhs_chunks[kc]
                    rd.append(lv)
                    mms.append((bvs[j].ap[:, 0:ntok], sv.ap[:, kk, j * 128:(j + 1) * 128], lap[:, 0:ntok], kc == 0, kc == nk - 1))
                mm(bvs[j], mms, rd)
        for j in range(4):
            evac(j, V(bvs[j].ap[:, 0:ntok], "ps", bvs[j].lo, bvs[j].hi))

    def projT(lhs_chunks, w_ap, k0, c0, evac, ntb=NTB):
        nk = len(lhs_chunks)
        bks = next_banks(ntb)
        bvs = [bank(b) for b in bks]
        for s0 in range(0, nk, 8):
            n = min(8, nk - s0)
            sv = wload(w_ap, k0 + s0 * 128, n, c0, 512)
            for tb in range(ntb):
                mms = []
                rd = [sv]
                for kk in range(n):
                    kc = s0 + kk
                    lv, lap = lhs_chunks[kc]
                    rd.append(lv)
                    mms.append((bvs[tb].ap, lap[:, tb * 128:(tb + 1) * 128], sv.ap[:, kk, :], kc == 0, kc == nk - 1))
                mm(bvs[tb], mms, rd)
        for tb in range(ntb):
            evac(tb, bvs[tb])

    hT_chunks = [(hT, hT.ap[:, c, :]) for c in range(32)]

    def sc(i):
        return sub(small, small.ap[:, i:i + 1], i * 4, i * 4 + 4)

    def rstd_from_sum(sumv, outv, n):
        ts(outv, sumv, 1.0 / n, EPS, ALU.mult, ALU.add)
        act(outv, outv, AF.Sqrt)
        o, i = outv.ap, outv.ap
        P.op("dve", lambda e: e.reciprocal(out=o, in_=i), reads=[outv], writes=[outv])

    def normA(xv, ss_i, rs_i):
        hb = hnb[nrm_cnt[0] % 2]
        nrm_cnt[0] += 1
        ssv = sc(ss_i)
        act(hb, xv, AF.Square, accum=ssv)
        rs = sc(rs_i)
        rstd_from_sum(ssv, rs, D)
        return (hb, rs, xv)

    def normB(st, tb, gidx):
        normB_id(st)
        normB_tr(st, tb, gidx)

    def normB_id(st):
        hb, rs, xv = st
        act(hb, xv, AF.Identity, scale=rs.ap, extra_r=[rs])

    def normB_tr(st, tb, gidx):
        hb, rs, xv = st
        for f0 in range(0, 32, 4):
            b = next_banks(1)[0]
            bv = bank(b)
            bt = bv.ap.rearrange("p (c t) -> p c t", c=4)
            mm(bv, [(bt[:, j, :], hb.ap[:, (f0 + j) * 128:(f0 + j + 1) * 128], ident, True, True) for j in range(4)], [hb, cbf])
            g_ap = gpre_s.ap[:, gidx * 32 + f0:gidx * 32 + f0 + 4].unsqueeze(2).to_broadcast([128, 4, 128])
            ov = V(hT.ap[:, f0:f0 + 4, tb * 128:(tb + 1) * 128], "sb", hT.lo, hT.hi)
            tt(ov, V(bt, "ps", bv.lo, bv.hi), V(g_ap, "sb", gpre_s.lo, gpre_s.hi), ALU.mult)

    def preload_x(row_src, ntb=NTB):
        for tb in range(min(ntb, len(xbufs))):
            dma("sp", xbufs[tb], row_src(tb))

    def norm_to_hT_single(row_src, gidx, ntb=NTB):
        for tb in range(ntb):
            dma("sp", xt, row_src(tb))
            st = normA(xt, 48 + (tb % 2), 50 + (tb % 2))
            normB(st, tb, gidx)

    def norm_to_hT(row_src, gidx, ntb=NTB, preloaded=False):
        xvs = []
        for tb in range(ntb):
            xv = xbufs[tb % len(xbufs)]
            if tb < len(xbufs) and not preloaded:
                dma("sp", xv, row_src(tb))
            xvs.append(xv)
        sts = {0: normA(xvs[0], 48, 50)}
        for tb in range(ntb):
            if tb + 1 < ntb:
                sts[tb + 1] = normA(xvs[tb + 1], 48 + ((tb + 1) % 2), 50 + ((tb + 1) % 2))
            normB(sts[tb], tb, gidx)
            if tb + len(xbufs) < ntb:
                dma("sp", xvs[tb], row_src(tb + len(xbufs)))

    def post_residual(src_rows, dst_rows, gi, next_gidx=None):
        dma("sp", gbc, dv(gpost[gi:gi + 1, :].partition_broadcast(128)[:, 0, :], "gpost"))
        rss = {}

        def s1(tb):
            ssv = sc(2 + tb)
            act(hnb[nrm_cnt[0] % 2], ytok[tb], AF.Square, accum=ssv)
            rs = sc(6 + tb)
            rstd_from_sum(ssv, rs, D)
            rss[tb] = rs

        def ld(tb):
            sv = src_rows(tb)
            for h in range(2):
                dma("sp", xth[h], V(sv.ap[:, h * 2048:(h + 1) * 2048], sv.space, sv.lo, sv.hi))

        def s2(tb):
            rs = rss[tb]
            for h in range(2):
                yh = V(ytok[tb].ap[:, h * 2048:(h + 1) * 2048], "sb", ytok[tb].lo + h * 8192, ytok[tb].lo + (h + 1) * 8192)
                gh = V(gbc.ap[:, h * 2048:(h + 1) * 2048], "sb", gbc.lo + h * 8192, gbc.lo + (h + 1) * 8192)
                tt(yh, yh, gh, ALU.mult)
                stt(yh, yh, rs.ap, xth[h], ALU.mult, ALU.add, extra_r=[rs])
            if tb + 1 < NTB:
                ld(tb + 1)
            dma("sp", dst_rows(tb), ytok[tb])

        ld(0)
        s1(0)
        if NTB > 1:
            s1(1)
        if next_gidx is None:
            for tb in range(NTB):
                s2(tb)
                if tb + 2 < NTB:
                    s1(tb + 2)
            return
        sts = {}
        for tb in range(NTB):
            s2(tb)
            if tb >= 1:
                normB(sts[tb - 1], tb - 1, next_gidx)
            hb = hnb[nrm_cnt[0] % 2]
            nrm_cnt[0] += 1
            chains = []
            if tb + 2 < NTB:
                ss1 = sc(2 + tb + 2)
                act(hb, ytok[tb + 2], AF.Square, accum=ss1)
                rss[tb + 2] = sc(6 + tb + 2)
                chains.append((ss1, rss[tb + 2]))
            ssA = sc(10 + tb)
            act(hb, ytok[tb], AF.Square, accum=ssA)
            rsA = sc(14 + tb)
            chains.append((ssA, rsA))
            for (a, b) in chains:
                ts(b, a, 1.0 / D, EPS, ALU.mult, ALU.add)
            for (a, b) in chains:
                act(b, b, AF.Sqrt)
            for (a, b) in chains:
                P.op("dve", lambda e, o=b.ap: e.reciprocal(out=o, in_=o), reads=[b], writes=[b])
            sts[tb] = (hb, rsA, ytok[tb])
        normB(sts[NTB - 1], NTB - 1, next_gidx)

    dma("sp", cbf, dv(c_bf, "c"))
    dma("sp", cf, dv(c_f32, "c"))
    dma("sp", gpre_s, dv(gpre, "c"))
    dma("sp", bg_s, dv(bgate, "c"))
    dma("sp", lng_s, dv(lng, "c"))

    U = [UBASE]

    def ualloc(n):
        off = U[0]
        U[0] += (n + 63) // 64 * 64
        assert U[0] <= POOLB, f"SBUF union overflow {U[0]}"
        return off

    U[0] = Y
    o_ws = ualloc(16 * 128 * 4)
    wsf = sb(o_ws, 16 * 128 * 4, F32, "p (g j) -> p g j", g=16)
    o_wsm = ualloc(16 * 128 * 2)
    wsm = sb(o_wsm, 16 * 128 * 2, BF16, "p (g j) -> p g j", g=16)
    dma("sp", wsf, dv(w_sp.rearrange("g i j -> i g j"), "wsp"))
    tril_b = V(tril.unsqueeze(1).to_broadcast([128, 16, 128]), "sb", cf.lo, cf.hi)
    tt(wsm, wsf, tril_b, ALU.mult)
    for g in range(16):
        b = next_banks(1)[0]
        bv = bank(b, 128, BF16)
        transp(bv, [(bv.ap, wsm.ap[:, g, :])], [wsm])
        cp(V(wsT.ap[:, g, :], "sb", wsT.lo + g * 256, wsT.lo + (g + 1) * 256), bv)
    o_r2 = ualloc(2048 * 4)
    rhs2 = sb(o_r2, 2048 * 4, F32, parts=2)
    o_l2 = ualloc(2048 * 4)
    lhs2 = sb(o_l2, 2048 * 4, F32, parts=2)
    o, = (lhs2.ap,)
    P.op("dve", lambda e: e.memset(lhs2.ap, 1.0), writes=[lhs2])
    dma("sp", V(pool[0:1, o_l2 // 2:(o_l2 + 8192) // 2].bitcast(F32), "sb", lhs2.lo, lhs2.hi), dv(lnb_row, "c"))
    dma("sp", V(pool[1:2, o_r2 // 2:(o_r2 + 8192) // 2].bitcast(F32), "sb", rhs2.lo, rhs2.hi), dv(bsp_row, "c"))
    for q in range(4):
        b = next_banks(1)[0]
        bv = bank(b)
        mm(bv, [(bv.ap[0:1, j * 128:(j + 1) * 128], ones[:, 0:1], wsT.ap[:, q * 4 + j, :], True, True) for j in range(4)],
           [wsT, cbf])
        cp(V(rhs2.ap[0:1, q * 512:(q + 1) * 512], "sb", rhs2.lo, rhs2.hi), V(bv.ap[0:1, :], "ps", bv.lo, bv.hi), eng="act")
    for g in range(16):
        b = next_banks(1)[0]
        bv = bank(b)
        mm(bv, [(bv.ap[:, 0:128], lhs2.ap[:, g * 128:(g + 1) * 128], rhs2.ap[:, g * 128:(g + 1) * 128], True, True)],
           [lhs2, rhs2])
        cp(V(Rs.ap[:, g, :], "sb", Rs.lo + g * 512, Rs.lo + (g + 1) * 512), V(bv.ap[:, 0:128], "ps", bv.lo, bv.hi))

    norm_to_hT(lambda tb: dv(memb[tb * 128:(tb + 1) * 128, :], "memb"), 3, ntb=2)

    def ev_km(j, bv):
        cp(V(kmT.ap[:, j, :], "sb", kmT.lo + j * 512, kmT.lo + (j + 1) * 512), bv, eng="act")
    projF(hT_chunks, w_xk, 0, ev_km, ntok=256)

    def ev_vm(tb, bv):
        cp(V(vm.ap[:, tb, :], "sb", vm.lo + tb * 1024, vm.lo + (tb + 1) * 1024), bv, eng="act")
    projT(hT_chunks, w_xv, 0, 0, ev_vm, ntb=2)

    vgs = [sb(Y + tb * 8192, 8192, F32) for tb in range(NTB)]
    o_uT = Y + NTB * 8192
    uT = sb(o_uT, 16 * TT * 2, BF16, "p (c t) -> p c t", c=16)
    o_vtok = o_uT + 16 * TT * 2
    vtok = sb(o_vtok, NTB * 2048 * 2, BF16, "p (b c) -> p b c", b=NTB)
    assert o_vtok + NTB * 2048 * 2 <= Y + NTB * D * 4
    U[0] = Y
    o_QT = ualloc(24 * TT * 2)
    QT = sb(o_QT, 24 * TT * 2, BF16, "p (c t) -> p c t", c=24)
    kwin = [sb(ualloc((W + TT) * 2), (W + TT) * 2) for (W, _d) in GROUPS]
    vwin = [sb(ualloc((W + TT) * 2), (W + TT) * 2, BF16, "p (b c) -> p b c", c=128) for (W, _d) in GROUPS]
    kraw = [sb(ualloc(TT * 2), TT * 2) for _ in range(4)]
    rtmp = [sb(ualloc(TT * 4), TT * 4, F32) for _ in range(4)]
    pT = [sb(ualloc(TT * 2), TT * 2) for _ in range(4)]
    rden = sb(ualloc(TT * 4), TT * 4, F32)
    vst = [sb(ualloc(512 * 2), 512 * 2) for _ in range(3)]
    assert U[0] <= Y + NTB * D * 4, "ytok-region overflow"
    assert 16 * TT * 2 <= D * 4
    o_ybT = o_xt
    ybT = sb(o_ybT, 16 * TT * 2, BF16, "p (c t) -> p c t", c=16)
    assert 4 * TT * 4 <= D * 2
    sa = sb(o_hn, 4 * TT * 4, F32, "p (c t) -> p c t", c=4)
    U0, U1, U2, U3 = UBASE, UBASE + 8192, UBASE + 16384, UBASE + 20480
    o_yaT = U0
    yaT = sb(U0, 8 * TT * 2, BF16, "p (c t) -> p c t", c=8)
    sbv = sb(U1, 4 * TT * 4, F32, "p (c t) -> p c t", c=4)
    mT = [sb(U2, 4 * TT * 2, BF16, "p (c t) -> p c t", c=4), sb(U3, 4 * TT * 2, BF16, "p (c t) -> p c t", c=4)]
    sgt = sb(U2, TT * 4, F32)
    angf = sb(U1, TT * 4, F32)
    angi = sb(U1 + TT * 4, TT * 4, I32)
    angk = sb(U1 + 2 * TT * 4, TT * 4, F32)
    gbc = sb(U1, D * 4, F32)
    ang_main = (angf, angi, angk)
    qxT = sb(U0, 4 * TT * 2, BF16, "p (c t) -> p c t", c=4)
    oxT = sb(U0 + 4 * TT * 2, 4 * TT * 2, BF16, "p (c t) -> p c t", c=4)
    pX = [sb(U1 + i * TT * 2, TT * 2) for i in range(4)]
    rdenX = sb(U1 + 4 * TT * 2, TT * 4, F32)
    aT = [sb(U0, 8 * TT * 2, BF16, "p (c t) -> p c t", c=8), sb(U1, 8 * TT * 2, BF16, "p (c t) -> p c t", c=8)]
    rl = [sb(U2 + i * TT * 4, TT * 4, F32) for i in range(3)]
    assert U2 + 3 * TT * 4 <= UBASE + USIZE and U1 + D * 4 <= UBASE + USIZE

    rr = {"kraw": 0, "rtmp": 0, "vst": 0, "pT": 0, "rl": 0}

    def rot(lst, key):
        v = lst[rr[key] % len(lst)]
        rr[key] += 1
        return v

    ang_alt = (sb(o_hn, TT * 4, F32), sb(o_hn + TT * 4, TT * 4, I32), sb(o_hn + 2 * TT * 4, TT * 4, F32))

    def rope_tables(tok0, alt=False):
        angf, angi, angk = ang_alt if alt else ang_main
        dma("sp", angi, dv(pos[0:1, tok0:tok0 + TT].partition_broadcast(128)[:, 0, :], "pos"))
        cp(angf, angi)
        ts(angf, angf, invf, None, ALU.mult, extra_r=[cf])
        for (dst, shift) in ((ropeS, 0.0), (ropeC, PI / 2)):
            ts(angk, angf, shift, 1.0 / TWO_PI, ALU.add, ALU.mult)
            cp(angi, angk)
            cp(angk, angi)
            stt(angk, angk, -TWO_PI, angf, ALU.mult, ALU.add)
            ts(angk, angk, shift, None, ALU.add)
            ts(angk, angk, PI, -PI, ALU.min, ALU.max)
            act(dst, angk, AF.Sin)

    def rope_evac(bv, dstv, dst_ap):
        kr = dstv
        act(V(dst_ap, dstv.space, dstv.lo, dstv.hi), bv, AF.Copy)
        b = next_banks(1)[0]
        rb = bank(b)
        mm(rb, [(rb.ap[0:32, 0:TT], ropeP[:, 0:32], dst_ap, True, True)], [kr, cbf])
        t1 = rot(rtmp, "rtmp")
        t2 = rot(rtmp, "rtmp")
        t1v = V(t1.ap[0:32, :], "sb", t1.lo, t1.hi)
        t2v = V(t2.ap[0:32, :], "sb", t2.lo, t2.hi)
        tt(t1v, V(rb.ap[0:32, 0:TT], "ps", rb.lo, rb.hi), V(ropeS.ap[0:32, :], "sb", ropeS.lo, ropeS.hi), ALU.mult)
        d32 = V(dst_ap[0:32, :], dstv.space, dstv.lo, dstv.hi)
        tt(t2v, d32, V(ropeC.ap[0:32, :], "sb", ropeC.lo, ropeC.hi), ALU.mult)
        tt(d32, t1v, t2v, ALU.add)

    def mixer_kv(tok0, glist):
        for g in glist:
            for half in range(2):
                c0 = 3072 + g * 1024 + half * 512

                def ev_k(j, bv, g=g, half=half):
                    kr = rot(kraw, "kraw")
                    rope_evac(bv, kr, kr.ap)
                    dma("sp", dv(kth[g, half * 4 + j, :, tok0:tok0 + TT], f"kth{g}"), kr)
                projF(hT_chunks, w_in, c0, ev_k)
            for half in range(2):
                c0 = 6144 + g * 1024 + half * 512

                def ev_v(tb, bv, g=g, half=half):
                    st = rot(vst, "vst")
                    cp(st, bv, eng="act")
                    dma("sp", dv(vh[g, tok0 + tb * 128:tok0 + (tb + 1) * 128, half * 512:(half + 1) * 512], f"vh{g}"), st)
                projT(hT_chunks, w_in, 0, c0, ev_v)

    def attention(tok0):
        S_BANKS = (0, 1, 2, 3)
        for h in range(8):
            Ob, Db = bank(6 - 2 * (h % 2)), bank(7 - 2 * (h % 2))
            blocks = []
            for g, (W, dil) in enumerate(GROUPS):
                nkb = W // 128
                kw, vw = kwin[g], vwin[g]
                ntok = W + TT
                kwv = V(kw.ap[:, 0:ntok], "sb", kw.lo, kw.hi)
                dma("sp", kwv, dv(kth[g, h, :, tok0 - W:tok0 + TT], f"kth{g}"))
                vwv = V(vw.ap[:, 0:ntok // 128, :], "sb", vw.lo, vw.hi)
                dma("sp", vwv, dv(vh[g, tok0 - W:tok0 + TT, h * 128:(h + 1) * 128].rearrange("(b p) c -> p b c", p=128), f"vh{g}"))
                for m in range(-nkb, NTB):
                    qlo, qhi = max(m, 0), min(m + nkb, NTB - 1)
                    blocks.append((g, dil, nkb, m, qlo, qhi, kwv, vwv))
            first = [True]
            pend = []

            def emit_pv(item):
                (g, dil, nkb, m, qlo, qhi, kwv, vwv, pt, n) = item
                st = first[0]
                first[0] = False
                mm(Ob, [(Ob.ap[:, qlo * 128:qlo * 128 + n], vwv.ap[:, m + nkb, :], pt.ap[:, 0:n], st, False)], [vwv, pt])
                mm(Db, [(Db.ap[:, qlo * 128:qlo * 128 + n], ones, pt.ap[:, 0:n], st, False)], [pt, cbf])

            for (g, dil, nkb, m, qlo, qhi, kwv, vwv) in blocks:
                n = (qhi - qlo + 1) * 128
                b = next_banks(1, S_BANKS)[0]
                Sb = bank(b)
                mms = [(Sb.ap[:, 0:n], kwv.ap[:, (m + nkb) * 128:(m + nkb + 1) * 128], QT.ap[:, g * 8 + h, qlo * 128:qlo * 128 + n], True, False)]
                for qb in range(qlo, qhi + 1):
                    dl = qb - m
                    typ = 0 if dl == 0 else (1 if dl == nkb else 2)
                    if not (typ == 2 and dil == 1):
                        mms.append((Sb.ap[:, (qb - qlo) * 128:(qb - qlo + 1) * 128], ident, maskap(g, typ), False, False))
                if tok0 + 128 * m < HALO:
                    for qb in range(qlo, qhi + 1):
                        mms.append((Sb.ap[:, (qb - qlo) * 128:(qb - qlo + 1) * 128], ident, halob, False, False))
                mm(Sb, mms, [kwv, QT, cbf])
                pt = rot(pT, "pT")
                act(V(pt.ap[:, 0:n], "sb", pt.lo, pt.hi), V(Sb.ap[:, 0:n], "ps", Sb.lo, Sb.hi), AF.Exp, scale=SCALE)
                pend.append((g, dil, nkb, m, qlo, qhi, kwv, vwv, pt, n))
                if len(pend) > 1:
                    emit_pv(pend.pop(0))
            while pend:
                emit_pv(pend.pop(0))
            P.op("dve", lambda e, o=rden.ap, i=Db.ap[:, 0:TT]: e.reciprocal(out=o, in_=i), reads=[Db], writes=[rden])
            tt(V(yaT.ap[:, h, :], "sb", yaT.lo + h * TT * 2, yaT.lo + (h + 1) * TT * 2), V(Ob.ap[:, 0:TT], "ps", Ob.lo, Ob.hi), rden, ALU.mult)

    def mixer_full(tok0):
        row0 = tok0 - HALO
        for q in range(4):
            def ev_v2(tb, bv, q=q):
                act(V(vgs[tb].ap[:, q * 512:(q + 1) * 512], "sb", vgs[tb].lo + q * 2048, vgs[tb].lo + (q + 1) * 2048), bv,
                    AF.Gelu_apprx_tanh)
            projT(hT_chunks, w_in, 0, 11264 + q * 512, ev_v2)
        for tb in range(NTB):
            s4 = V(small.ap[:, 16 + tb * 4:20 + tb * 4], "sb", small.lo + (16 + tb * 4) * 4, small.lo + (20 + tb * 4) * 4)
            mean = sc(32 + tb)
            act(V(hn.ap[:, 0:2048], 'sb', hn.lo, hn.hi), vgs[tb], AF.Identity, accum=mean)
            ts(mean, mean, -1.0 / 2048, None, ALU.mult)
            ssq = sc(36 + tb)
            act(V(hn.ap[:, 0:2048], 'sb', hn.lo, hn.hi), vgs[tb], AF.Square, accum=ssq)
            var = sc(40 + tb)
            m2 = sc(44 + tb)
            ts(m2, mean, mean.ap, -1.0, ALU.mult, ALU.mult, extra_r=[mean])
            ts(var, ssq, 1.0 / 2048, EPS, ALU.mult, ALU.add)
            act(var, var, AF.Sqrt, bias=m2.ap, extra_r=[m2])
            P.op("dve", lambda e, o=var.ap: e.reciprocal(out=o, in_=o), reads=[var], writes=[var])
            ts(V(vtok.ap[:, tb, :], "sb", vtok.lo + tb * 4096, vtok.lo + (tb + 1) * 4096), vgs[tb], mean.ap, var.ap,
               ALU.add, ALU.mult, extra_r=[mean, var])
        for q in range(4):
            def ev_u(j, bv, q=q):
                c = q * 4 + j
                act(V(uT.ap[:, c, :], "sb", uT.lo + c * TT * 2, uT.lo + (c + 1) * TT * 2), bv, AF.Gelu_apprx_tanh)
            projF(hT_chunks, w_in, 9216 + q * 512, ev_u)
        for cg in range(16):
            b = next_banks(1)[0]
            bv = bank(b)
            mm(bv, [(bv.ap[:, tb * 128:(tb + 1) * 128], vtok.ap[:, tb, cg * 128:(cg + 1) * 128], wsT.ap[:, cg, :], True, True)
                    for tb in range(NTB)], [vtok, wsT])
            b3 = bv.ap[:, 0:TT].rearrange("p (b i) -> p b i", b=NTB)
            r3 = Rs.ap[:, cg, :].unsqueeze(1).to_broadcast([128, NTB, 128])
            s3 = sgt.ap.rearrange("p (b i) -> p b i", b=NTB)
            stt(V(s3, "sb", sgt.lo, sgt.hi), V(b3, "ps", bv.lo, bv.hi), lng_s.ap[:, cg:cg + 1], V(r3, "sb", Rs.lo, Rs.hi),
                ALU.mult, ALU.add, extra_r=[lng_s])
            tt(V(ybT.ap[:, cg, :], "sb", ybT.lo + cg * TT * 2, ybT.lo + (cg + 1) * TT * 2), sgt,
               V(uT.ap[:, cg, :], "sb", uT.lo + cg * TT * 2, uT.lo + (cg + 1) * TT * 2), ALU.mult)
        for g in range(3):
            for half in range(2):
                def ev_q(j, bv, g=g, half=half):
                    c = g * 8 + half * 4 + j
                    rope_evac(bv, V(QT.ap, "sb", QT.lo + c * TT * 2, QT.lo + (c + 1) * TT * 2), QT.ap[:, c, :])
                projF(hT_chunks, w_in, g * 1024 + half * 512, ev_q)
        attention(tok0)
        if dbg.get("dump") and tok0 == HALO:
            dma("sp", dv(dbgf, "dbgf"), small)
            dma("sp", dv(dbgo[:, 0:2048], "dbg"), V(pool[:, o_yaT // 2:o_yaT // 2 + 2048], "sb", yaT.lo, yaT.hi))
            dma("sp", dv(dbgo[:, 2048:6144], "dbg"), V(pool[:, o_ybT // 2:o_ybT // 2 + 4096], "sb", ybT.lo, ybT.hi))
            dma("sp", dv(dbgo[:, 6144:10240], "dbg"), V(pool[:, o_uT // 2:o_uT // 2 + 4096], "sb", uT.lo, uT.hi))
            dma("sp", dv(dbgo[:, 10240:16384], "dbg"), V(pool[:, o_QT // 2:o_QT // 2 + 6144], "sb", QT.lo, QT.hi))
        ya_chunks = [(yaT, yaT.ap[:, c, :]) for c in range(8)]
        yb_chunks = [(ybT, ybT.ap[:, c, :]) for c in range(16)]
        for cg in range(8):
            def ev_ga(j, bv, cg=cg):
                act(V(sa.ap[:, j, :], "sb", sa.lo + j * TT * 4, sa.lo + (j + 1) * TT * 4), bv, AF.Sigmoid,
                    bias=bg_s.ap[:, cg * 4 + j:cg * 4 + j + 1], extra_r=[bg_s])
            projF(hT_chunks, w_gate, cg * 512, ev_ga)

            def ev_gb(j, bv, cg=cg):
                act(V(sbv.ap[:, j, :], "sb", sbv.lo + j * TT * 4, sbv.lo + (j + 1) * TT * 4), bv, AF.Sigmoid,
                    bias=bg_s.ap[:, 32 + cg * 4 + j:32 + cg * 4 + j + 1], extra_r=[bg_s])
            projF(hT_chunks, w_gate, D + cg * 512, ev_gb)

            def ev_a(j, bv):
                v_ = V(sa.ap[:, j, :], "sb", sa.lo + j * TT * 4, sa.lo + (j + 1) * TT * 4)
                tt(v_, bv, v_, ALU.mult)
            projF(ya_chunks, w_ba, cg * 512, ev_a)
            mt = mT[cg % 2]

            def ev_b(j, bv, mt=mt):
                v_ = V(sbv.ap[:, j, :], "sb", sbv.lo + j * TT * 4, sbv.lo + (j + 1) * TT * 4)
                tt(v_, bv, v_, ALU.mult)
                tt(V(mt.ap[:, j, :], "sb", mt.lo + j * TT * 2, mt.lo + (j + 1) * TT * 2),
                   V(sa.ap[:, j, :], "sb", sa.lo + j * TT * 4, sa.lo + (j + 1) * TT * 4), v_, ALU.add)
            projF(yb_chunks, w_bb, cg * 512, ev_b)
            m_chunks = [(mt, mt.ap[:, c, :]) for c in range(4)]
            for ocg in range(8):
                def ev_o(tb, bv, ocg=ocg, cg=cg):
                    yv = V(ytok[tb].ap[:, ocg * 512:(ocg + 1) * 512], "sb", ytok[tb].lo + ocg * 2048, ytok[tb].lo + (ocg + 1) * 2048)
                    if cg == 0:
                        cp(yv, bv, eng="act")
                    else:
                        tt(yv, bv, yv, ALU.add)
                projT(m_chunks, w_out, cg * 512, ocg * 512, ev_o)
        post_residual(lambda tb: dv(xs[tok0 + tb * 128:tok0 + (tb + 1) * 128, :], "xs"),
                      lambda tb: dv(out[row0 + tb * 128:row0 + (tb + 1) * 128, :], f"out{row0 + tb * 128}"), 0, next_gidx=1)

    def xa_phase(row0):
        rows = lambda tb: dv(out[row0 + tb * 128:row0 + (tb + 1) * 128, :], f"out{row0 + tb * 128}")

        def ev_q(j, bv):
            cp(V(qxT.ap[:, j, :], "sb", qxT.lo + j * TT * 2, qxT.lo + (j + 1) * TT * 2), bv, eng="act")
        projF(hT_chunks, w_xq, 0, ev_q)
        for h in range(4):
            Ob, Db = bank(6 - 2 * (h % 2)), bank(7 - 2 * (h % 2))
            pts = []
            for mb in range(2):
                b = next_banks(1, (0, 1, 2, 3))[0]
                Sb = bank(b)
                mm(Sb, [(Sb.ap[:, 0:TT], kmT.ap[:, h, mb * 128:(mb + 1) * 128], qxT.ap[:, h, :], True, True)], [kmT, qxT])
                pt = rot(pX, "pT")
                act(pt, V(Sb.ap[:, 0:TT], "ps", Sb.lo, Sb.hi), AF.Exp, scale=SCALE)
                pts.append(pt)
            for mb in range(2):
                mm(Ob, [(Ob.ap[:, 0:TT], vm.ap[:, mb, h * 128:(h + 1) * 128], pts[mb].ap, mb == 0, mb == 1)], [vm, pts[mb]])
                mm(Db, [(Db.ap[:, 0:TT], ones, pts[mb].ap, mb == 0, mb == 1)], [pts[mb], cbf])
            P.op("dve", lambda e, o=rdenX.ap, i=Db.ap[:, 0:TT]: e.reciprocal(out=o, in_=i), reads=[Db], writes=[rdenX])
            tt(V(oxT.ap[:, h, :], "sb", oxT.lo + h * TT * 2, oxT.lo + (h + 1) * TT * 2), V(Ob.ap[:, 0:TT], "ps", Ob.lo, Ob.hi), rdenX, ALU.mult)
        ox_chunks = [(oxT, oxT.ap[:, c, :]) for c in range(4)]
        for ocg in range(8):
            def ev_o(tb, bv, ocg=ocg):
                cp(V(ytok[tb].ap[:, ocg * 512:(ocg + 1) * 512], "sb", ytok[tb].lo + ocg * 2048, ytok[tb].lo + (ocg + 1) * 2048), bv, eng="act")
            projT(ox_chunks, w_xo, 0, ocg * 512, ev_o)
        post_residual(rows, rows, 1, next_gidx=2)

    def mlp_phase(row0, early_next=None):
        rows = lambda tb: dv(out[row0 + tb * 128:row0 + (tb + 1) * 128, :], f"out{row0 + tb * 128}")
        for sec in range(DFF // SEC):
            at = aT[sec % 2]
            if sec == DFF // SEC - 1 and early_next is not None:
                early_next["pre"]()
            for half in range(SEC // 512):
                def ev_up(j, bv, half=half, at=at):
                    r = rot(rl, "rl")
                    act(r, bv, AF.Relu)
                    c = half * 4 + j
                    tt(V(at.ap[:, c, :], "sb", at.lo + c * TT * 2, at.lo + (c + 1) * TT * 2), r, r, ALU.mult)
                projF(hT_chunks, w_up, sec * SEC + half * 512, ev_up)
            a_chunks = [(at, at.ap[:, c, :]) for c in range(SEC // 128)]
            if sec == DFF // SEC - 1 and early_next is not None:
                early_next["step"](0)
            for ocg in range(8):
                def ev_dn(tb, bv, ocg=ocg, sec=sec):
                    yv = V(ytok[tb].ap[:, ocg * 512:(ocg + 1) * 512], "sb", ytok[tb].lo + ocg * 2048, ytok[tb].lo + (ocg + 1) * 2048)
                    if sec == 0:
                        cp(yv, bv, eng="act")
                    else:
                        tt(yv, bv, yv, ALU.add)
                projT(a_chunks, w_down, sec * SEC, ocg * 512, ev_dn)
                if sec == DFF // SEC - 1 and early_next is not None and ocg in (1, 3, 5) and (ocg + 1) // 2 < NTB:
                    early_next["step"]((ocg + 1) // 2)
        post_residual(rows, rows, 2)

    state = {}
    for t in range(NHT + NT):
        tok0 = t * TT
        is_halo = t < NHT
        if is_halo:
            glist = [g for g, (W, dil) in enumerate(GROUPS) if HALO - (tok0 + TT) < W]
        else:
            glist = [0, 1, 2]
        xrows = lambda tb, tok0=tok0: dv(xs[tok0 + tb * 128:tok0 + (tb + 1) * 128, :], "xs")
        if not state.get("prenormed"):
            rope_tables(tok0)
            norm_to_hT(xrows, 0, preloaded=state.get("preloaded", False))
        state["prenormed"] = False
        state["preloaded"] = False
        if is_halo and t + 1 < NHT + NT:
            ntok0 = (t + 1) * TT
            preload_x(lambda tb, ntok0=ntok0: dv(xs[ntok0 + tb * 128:ntok0 + (tb + 1) * 128, :], "xs"))
            state["preloaded"] = True
        mixer_kv(tok0, glist)
        if not is_halo:
            mixer_full(tok0)
            if dbg.get("stop") == "mixer":
                break
            xa_phase(tok0 - HALO)
            if dbg.get("stop") == "xa":
                break
            early = None
            if t + 1 < NHT + NT and (not dbg.get("ntiles") or (t - NHT + 1) < dbg["ntiles"]):
                ntok0 = (t + 1) * TT

                def mk_early(ntok0):
                    xr = lambda tb: dv(xs[ntok0 + tb * 128:ntok0 + (tb + 1) * 128, :], "xs")
                    est = {}

                    def pre():
                        rope_tables(ntok0, alt=True)
                        dma("sp", xt, xr(0))
                        est[0] = normA(xt, 48, 50)
                        normB_id(est[0])
                        if NTB > 1:
                            dma("sp", xt, xr(1))

                    def step(tb):
                        normB_tr(est[tb], tb, 0)
                        if tb + 1 < NTB:
                            est[tb + 1] = normA(xt, 48 + ((tb + 1) % 2), 50 + ((tb + 1) % 2))
                            normB_id(est[tb + 1])
                            if tb + 2 < NTB:
                                dma("sp", xt, xr(tb + 2))
                        else:
                            state["prenormed"] = True
                    return {"pre": pre, "step": step}
                early = mk_early(ntok0)
            mlp_phase(tok0 - HALO, early)
            if dbg.get("ntiles") and t - NHT + 1 >= dbg["ntiles"]:
                break

    final_waits = [(d.sem, d.count) for d in P.dsems + wsems if d.count > 0]

    with nc.Block() as block:
        def replay(e, eng):
            for waits, fn, sem, inc in eng.prog:
                for (s, v) in waits:
                    e.wait_ge(s, v)
                ins = fn(e)
                ins.then_inc(sem, inc)

        @block.tensor
        def _(e):
            replay(e, P.engs["pe"])

        @block.scalar
        def _(e):
            replay(e, P.engs["act"])

        @block.vector
        def _(e):
            replay(e, P.engs["dve"])

        @block.gpsimd
        def _(e):
            replay(e, P.engs["pool"])

        @block.sync
        def _(e):
            replay(e, P.engs["sp"])
            for (s, v) in final_waits:
                e.wait_ge(s, v)
            for nm in ("pe", "act", "dve", "pool"):
                en = P.engs[nm]
                if en.cnt:
                    e.wait_ge(en.sem, en.cnt)
    es.close()
    return nc


def _consts(core):
    bf = ml_dtypes.bfloat16
    c = np.zeros((128, 13, 128), np.float32)
    c[:, 0] = np.eye(128)
    c[:, 1] = 1.0
    for i in range(16):
        c[i + 16, 2, i] = -1.0
        c[i, 2, i + 16] = 1.0
    k = np.arange(128)[:, None]
    q = np.arange(128)[None, :]
    for g, (W, dil) in enumerate(GROUPS):
        res = ((q - k) % dil) == 0
        c[:, 3 + g * 3 + 0] = np.where(res & (q - k >= 0), 0.0, NEGM)
        c[:, 3 + g * 3 + 1] = np.where(res & (q - k <= 0), 0.0, NEGM)
        c[:, 3 + g * 3 + 2] = np.where(res, 0.0, NEGM)
    c[:, 12] = NEGM if core % 2 == 0 else 0.0
    cf = np.zeros((128, 129), np.float32)
    cf[:, :128] = np.tril(np.ones((128, 128), np.float32))
    inv = (np.float32(500000.0) ** (-np.arange(0, 32, 2, dtype=np.float32) / np.float32(32))).astype(np.float32)
    cf[0:16, 128] = inv
    cf[16:32, 128] = inv
    return c.reshape(128, 13 * 128).astype(bf), cf


_NC_CACHE = {}


def kernel(x, mem, positions, mix_pre_g, w_in, sgu_ln_g, sgu_ln_b, w_spatial, b_spatial,
           w_branch_a, w_branch_b, w_gate, b_gate, w_out, mix_post_g, xa_pre_g, mem_norm_g,
           w_xq, w_xk, w_xv, w_xo, xa_post_g, mlp_pre_g, w_up, w_down, mlp_post_g):
    f = lambda a: np.ascontiguousarray(np.asarray(a))
    x = np.asarray(x)
    positions = np.asarray(positions)
    col = lambda g: np.asarray(g)[0].reshape(-1, 128).T
    gpre = f(np.concatenate([col(mix_pre_g), col(xa_pre_g), col(mlp_pre_g), col(mem_norm_g)], axis=1).astype(np.float32))
    gpost = f(np.stack([np.asarray(mix_post_g)[0], np.asarray(xa_post_g)[0], np.asarray(mlp_post_g)[0]]).astype(np.float32))
    shared = {
        "w_in": f(np.asarray(w_in)[0]), "w_gate": f(np.asarray(w_gate)[0]), "w_branch_a": f(np.asarray(w_branch_a)[0]),
        "w_branch_b": f(np.asarray(w_branch_b)[0]), "w_out": f(np.asarray(w_out)[0]), "w_xq": f(np.asarray(w_xq)[0]),
        "w_xk": f(np.asarray(w_xk)[0]), "w_xv": f(np.asarray(w_xv)[0]), "w_xo": f(np.asarray(w_xo)[0]),
        "w_up": f(np.asarray(w_up)[0]), "w_down": f(np.asarray(w_down)[0]), "w_spatial": f(np.asarray(w_spatial)[0]),
        "gpre": gpre, "gpost": gpost, "bgate": f(col(b_gate)), "lng": f(col(sgu_ln_g)),
        "lnb_row": f(np.asarray(sgu_ln_b)[0].reshape(1, 2048)), "bsp_row": f(np.asarray(b_spatial)[0].reshape(1, 2048)),
    }
    in_maps = []
    for c in range(8):
        b, half = c // 2, c % 2
        own = x[b, half * OWN:(half + 1) * OWN]
        if half == 1:
            halo = x[b, 0:HALO]
            ph = positions[b, 0:HALO]
        else:
            halo = np.zeros((HALO, D), np.float32)
            ph = np.zeros((HALO,), np.int32)
        cb, cf = _consts(c)
        m = dict(shared)
        m["xs"] = f(np.concatenate([halo, own], axis=0))
        m["pos"] = f(np.concatenate([ph, positions[b, half * OWN:(half + 1) * OWN]]).reshape(1, -1).astype(np.int32))
        m["memb"] = f(np.asarray(mem)[b])
        m["c_bf"] = cb
        m["c_f32"] = cf
        in_maps.append(m)
    dbg = _NC_CACHE.get("dbg")
    if "nc" not in _NC_CACHE:
        _NC_CACHE["nc"] = build_program(dbg)
    nc = _NC_CACHE["nc"]
    ncores = (dbg or {}).get("ncores", 8)
    res = run_bass_kernel_spmd(nc, in_maps[:ncores], core_ids=list(range(ncores)))
    if ncores < 8:
        _NC_CACHE["last"] = res.results[0]
        return res.results[0]["out"]
    outp = np.empty((4, 4096, D), np.float32)
    for c in range(8):
        b, half = c // 2, c % 2
        outp[b, half * OWN:(half + 1) * OWN] = res.results[c]["out"]
    return outp
```
